# Optimizing a Trainium2 kernel written in Bass

```python
import jax, jax.numpy as jnp
from jax import lax
import numpy as np

D_MODEL = 1024
BATCH = 16
SEQ = 2048
DEPTH = 4

RWKV_HEAD = 64
RWKV_WIDTH = 1024
RWKV_HEADS = RWKV_WIDTH // RWKV_HEAD
DECAY_LORA = 64
ICLR_LORA = 64
VRES_LORA = 32
RWKV_SHIFT_WIDTH = 3 * RWKV_WIDTH + 2 * DECAY_LORA + 2 * ICLR_LORA
RET_HEADS = 8
RET_QK_HEAD = 64
RET_V_HEAD = 128
RET_QK_WIDTH = RET_HEADS * RET_QK_HEAD
RET_V_WIDTH = RET_HEADS * RET_V_HEAD
RET_CHUNK = 128
ROPE_BASE = 10000.0
N_IN = RWKV_SHIFT_WIDTH + RWKV_WIDTH + 2 * RET_QK_WIDTH + 2 * RET_V_WIDTH + 2 * D_MODEL
NORM_EPS = 1e-6
GN_EPS = 64e-5

kernel_name = "hybrid_rwkv7_retention_gated_encoder"


def _cut_points(widths):
    pts, s = [], 0
    for w in widths[:-1]:
        s += w
        pts.append(s)
    return pts


def rmsnorm(x, g):
    xf = x.astype(jnp.float32)
    y = xf * lax.rsqrt(jnp.mean(xf * xf, axis=-1, keepdims=True) + NORM_EPS)
    return (y * g.astype(jnp.float32)).astype(x.dtype)


def head_norm(y, eps):
    mu = jnp.mean(y, axis=-1, keepdims=True)
    var = jnp.mean(jnp.square(y - mu), axis=-1, keepdims=True)
    yn = (y - mu) * lax.rsqrt(var + eps)
    return yn.reshape(y.shape[0], y.shape[1], -1)


def centred_shift(p, mu_prev, mu_next):
    prev = jnp.pad(p[:, :-1], ((0, 0), (1, 0), (0, 0)))
    nxt = jnp.pad(p[:, 1:], ((0, 0), (0, 1), (0, 0)))
    return p + mu_prev * (prev - p) + mu_next * (nxt - p)


def wkv7_scan(r, w, k, v, a, b, reverse):
    B, _, H, N = r.shape

    def step(S, inp):
        r_t, w_t, k_t, v_t, a_t, b_t = inp
        sa = jnp.einsum('bhvk,bhk->bhv', S, a_t)
        S = S * w_t[:, :, None, :] + sa[..., None] * b_t[:, :, None, :] + v_t[..., None] * k_t[:, :, None, :]
        y = jnp.einsum('bhvk,bhk->bhv', S, r_t)
        return S, y

    xs = tuple(jnp.moveaxis(t, 1, 0) for t in (r, w, k, v, a, b))
    S0 = jnp.zeros((B, H, N, N), jnp.float32)
    _, ys = lax.scan(step, S0, xs, reverse=reverse)
    return jnp.moveaxis(ys, 0, 1)


def rwkv7_branch(r, k, v, dec_lo, iclr_lo, w_decay_up, decay_bias, w_iclr_up, iclr_bias,
                 k_k, k_a, r_k, lnx_gain, lnx_bias):
    f32 = jnp.float32
    r, k, v = r.astype(f32), k.astype(f32), v.astype(f32)
    B, S, _ = r.shape
    hs = lambda t: t.reshape(B, S, RWKV_HEADS, RWKV_HEAD)
    kk = hs(k * k_k)
    kk = kk * lax.rsqrt(jnp.sum(kk * kk, axis=-1, keepdims=True) + 1e-12)
    rh, vh = hs(r), hs(v)
    y = jnp.zeros_like(rh)
    bonus = jnp.zeros_like(rh)
    for d in range(2):
        decay_logit = -jax.nn.softplus(-(decay_bias[d] + jnp.tanh(dec_lo[d].astype(f32)) @ w_decay_up[d])) - 0.5
        w = jnp.exp(-jnp.exp(decay_logit))
        a = jax.nn.sigmoid(iclr_bias[d] + iclr_lo[d].astype(f32) @ w_iclr_up[d])
        kd = k * (1.0 + (a - 1.0) * k_a)
        ah, kdh = hs(a), hs(kd)
        y = y + wkv7_scan(rh, hs(w), kdh, vh, -kk, kk * ah, reverse=(d == 1))
        bonus = bonus + jnp.sum(rh * kdh * r_k, axis=-1, keepdims=True) * vh
    return head_norm(y, GN_EPS) * lnx_gain + lnx_bias + bonus.reshape(B, S, -1)


def rotary(x, pos):
    half = x.shape[-1] // 2
    freqs = jnp.power(ROPE_BASE, -jnp.arange(half, dtype=jnp.float32) / half)
    ang = pos[:, None] * freqs[None, :]
    cos = jnp.cos(ang)[None, :, None, :]
    sin = jnp.sin(ang)[None, :, None, :]
    x1, x2 = x[..., :half], x[..., half:]
    return jnp.concatenate([x1 * cos - x2 * sin, x1 * sin + x2 * cos], axis=-1)


def retention_chunkwise(q, k, v, log_g, strict):
    f32 = jnp.float32
    B, H, S, dk = q.shape
    dv = v.shape[-1]
    C = RET_CHUNK
    NC = S // C
    qc = q.reshape(B, H, NC, C, dk)
    kc = k.reshape(B, H, NC, C, dk)
    vc = v.reshape(B, H, NC, C, dv)
    idx = jnp.arange(C)
    diff = idx[:, None] - idx[None, :]
    mask = (diff > 0) if strict else (diff >= 0)
    dmat = jnp.where(mask[None], jnp.exp(jnp.where(mask, diff, 0)[None].astype(f32) * log_g[:, None, None]), 0.0)
    scores = jnp.einsum('bhncd,bhnmd->bhncm', qc, kc) * dmat[None, :, None]
    intra = jnp.einsum('bhncm,bhnme->bhnce', scores, vc)
    posf = idx.astype(f32)
    xi = jnp.exp((posf + 1.0) * log_g[:, None])
    zeta = jnp.exp((C - 1.0 - posf) * log_g[:, None])
    kv = jnp.einsum('bhncd,hc,bhnce->bhnde', kc, zeta, vc)
    chunk_decay = jnp.exp(C * log_g)[None, :, None, None]

    def step(state, kv_j):
        return state * chunk_decay + kv_j, state

    _, s_prev = lax.scan(step, jnp.zeros((B, H, dk, dv), f32), jnp.moveaxis(kv, 2, 0))
    s_prev = jnp.moveaxis(s_prev, 0, 2)
    cross = jnp.einsum('bhncd,bhnde->bhnce', qc, s_prev) * xi[None, :, None, :, None]
    return (intra + cross).reshape(B, H, S, dv)


def retention_branch(q, k, v, gain):
    f32 = jnp.float32
    B, S, _ = q.shape
    q = q.astype(f32).reshape(B, S, RET_HEADS, RET_QK_HEAD)
    k = k.astype(f32).reshape(B, S, RET_HEADS, RET_QK_HEAD)
    v = v.astype(f32).reshape(B, S, RET_HEADS, RET_V_HEAD)
    pos = jnp.arange(S, dtype=f32)
    q = rotary(q, pos) * (RET_QK_HEAD ** -0.5)
    k = rotary(k, pos)
    q, k, v = (t.transpose(0, 2, 1, 3) for t in (q, k, v))
    log_g = jnp.log1p(-jnp.exp2(-5.0 - jnp.arange(RET_HEADS, dtype=f32)))
    fwd = retention_chunkwise(q, k, v, log_g, strict=False)
    bwd = jnp.flip(retention_chunkwise(jnp.flip(q, 2), jnp.flip(k, 2), jnp.flip(v, 2), log_g, strict=True), 2)
    y = (fwd + bwd).transpose(0, 2, 1, 3)
    return head_norm(y, NORM_EPS) * gain


def setup_inputs(seed: int = 0) -> dict:
    key = jax.random.key(seed)
    ks = jax.random.split(key, 32)
    n = jax.random.normal
    f32 = jnp.float32
    L, D, DA, DB = DEPTH, D_MODEL, RWKV_WIDTH, RET_V_WIDTH
    decay_base = jnp.linspace(-5.0, 1.0, DA, dtype=f32)
    return {
        "x": n(ks[0], (BATCH, SEQ, D), f32),
        "norm_gain": 1.0 + 0.02 * n(ks[1], (L, D), f32),
        "w_in": n(ks[2], (L, D, N_IN), f32) * D ** -0.5,
        "w_vres_down": n(ks[3], (L - 1, D, VRES_LORA), f32) * D ** -0.5,
        "shift_prev": jax.random.uniform(ks[4], (L, RWKV_SHIFT_WIDTH), f32, 0.1, 0.6),
        "shift_next": jax.random.uniform(ks[5], (L, RWKV_SHIFT_WIDTH), f32, 0.1, 0.6),
        "w_decay_up": n(ks[6], (L, 2, DECAY_LORA, DA), f32) * 0.1,
        "decay_bias": decay_base + 0.3 * n(ks[7], (L, 2, DA), f32),
        "w_iclr_up": n(ks[8], (L, 2, ICLR_LORA, DA), f32) * 0.1,
        "iclr_bias": 0.1 * n(ks[9], (L, 2, DA), f32),
        "w_vres_up": n(ks[10], (L - 1, VRES_LORA, DA), f32) * 0.1,
        "vres_bias": 0.1 * n(ks[11], (L - 1, DA), f32),
        "k_k": 0.85 + 0.02 * n(ks[12], (L, DA), f32),
        "k_a": 1.0 + 0.02 * n(ks[13], (L, DA), f32),
        "r_k": 0.1 * n(ks[14], (L, RWKV_HEADS, RWKV_HEAD), f32),
        "lnx_gain": 1.0 + 0.02 * n(ks[15], (L, DA), f32),
        "lnx_bias": 0.02 * n(ks[16], (L, DA), f32),
        "w_branch_a": n(ks[17], (L, DA, D), f32) * DA ** -0.5,
        "ret_norm_gain": 1.0 + 0.02 * n(ks[18], (L, DB), f32),
        "w_branch_b": n(ks[19], (L, DB, D), f32) * DB ** -0.5,
        "w_out": n(ks[20], (L, D, D), f32) * D ** -0.5,
        "final_gain": 1.0 + 0.02 * n(ks[21], (D,), f32),
    }


def reference(x, norm_gain, w_in, w_vres_down, shift_prev, shift_next, w_decay_up, decay_bias,
              w_iclr_up, iclr_bias, w_vres_up, vres_bias, k_k, k_a, r_k, lnx_gain, lnx_bias,
              w_branch_a, ret_norm_gain, w_branch_b, w_out, final_gain):
    shift_cuts = _cut_points([RWKV_WIDTH] * 3 + [DECAY_LORA] * 2 + [ICLR_LORA] * 2)
    rest_cuts = _cut_points([RWKV_WIDTH, RET_QK_WIDTH, RET_QK_WIDTH, RET_V_WIDTH, RET_V_WIDTH,
                             D_MODEL, D_MODEL, VRES_LORA])
    v_first = None
    for l in range(DEPTH):
        hn = rmsnorm(x, norm_gain[l])
        w_l = w_in[l] if l == 0 else jnp.concatenate([w_in[l], w_vres_down[l - 1]], axis=1)
        proj = hn @ w_l
        shifted = centred_shift(proj[..., :RWKV_SHIFT_WIDTH], shift_prev[l], shift_next[l])
        r, k, v, dec_f, dec_b, iclr_f, iclr_b = jnp.split(shifted, shift_cuts, axis=-1)
        gate_a, q_b, k_b, v_b, gate_b, mg_a, mg_b, vres_lo = jnp.split(proj[..., RWKV_SHIFT_WIDTH:], rest_cuts, axis=-1)
        if l == 0:
            v_first = v
        else:
            v = v + (v_first - v) * jax.nn.sigmoid(vres_bias[l - 1] + vres_lo @ w_vres_up[l - 1])
        y_a = rwkv7_branch(r, k, v, (dec_f, dec_b), (iclr_f, iclr_b), w_decay_up[l], decay_bias[l],
                           w_iclr_up[l], iclr_bias[l], k_k[l], k_a[l], r_k[l], lnx_gain[l], lnx_bias[l])
        y_a = y_a.astype(x.dtype) * jax.nn.silu(gate_a)
        y_b = retention_branch(q_b, k_b, v_b, ret_norm_gain[l]).astype(x.dtype) * jax.nn.silu(gate_b)
        merged = jax.nn.sigmoid(mg_a) * (y_a @ w_branch_a[l]) + jax.nn.sigmoid(mg_b) * (y_b @ w_branch_b[l])
        x = x + merged @ w_out[l]
    return rmsnorm(x, final_gain)
```

```python
import math
import os
from contextlib import ExitStack

import numpy as np
import concourse.bass as bass
import concourse.mybir as mybir
from concourse.bass_utils import run_bass_kernel_spmd

F32 = mybir.dt.float32
BF16 = mybir.dt.bfloat16
ALU = mybir.AluOpType
AF = mybir.ActivationFunctionType
AX = mybir.AxisListType

NCORE = 8
DEPTH = 4
D = 1024
SEQ = 2048
NSEQ = 2
NTOK = NSEQ * SEQ
NT = NTOK // 128
NIN = 9472
NINX = NIN + 32
SHW = 3328
C_R, C_K, C_V, C_CODE = 0, 1024, 2048, 3072
C_GA, C_QB, C_KB, C_VB, C_GB, C_MA, C_MB, C_VL = 3328, 4352, 4864, 5376, 6400, 7424, 8448, 9472
NORM_EPS = 1e-6
GN_EPS = 64e-5
TB = 32


class Buf:
    def __init__(self, t, name=""):
        self.t = t
        self.name = name
        self.w = {}
        self.r = {}
        self.dsem = None


class FW:
    SEM_MAX = 30000

    def __init__(self, nc):
        self.nc = nc
        self.eng = {"pe": nc.tensor, "act": nc.scalar, "dve": nc.vector, "pool": nc.gpsimd, "sp": nc.sync}
        self.sems = {}
        self.cnt = {}
        self.esem = {}
        self.seen = {e: {} for e in self.eng}
        self.nsem = 0
        self.n_wait = 0
        self.n_ins = 0
        self.free_dsems = []
        self.phase_bufs = []
        self.stack = None
        self.sb_bytes = 0
        self.mute = False
        self.skip = ()
        for e in ("pe", "act", "dve", "pool"):
            self.esem[e] = self.new_sem("e" + e)

    def new_sem(self, name):
        key = f"{name}_{self.nsem}"
        self.nsem += 1
        self.sems[key] = self.nc.alloc_semaphore(name=key)
        self.cnt[key] = 0
        return key

    def _get_dsem(self):
        while self.free_dsems:
            k = self.free_dsems.pop()
            if self.cnt[k] + 16 * 64 < self.SEM_MAX:
                return k
        return self.new_sem("d")

    def begin_phase(self, name=""):
        self.mute = name in self.skip
        self.stack = ExitStack()
        self.phase_bufs = []
        self.sb_bytes = 0

    def end_phase(self):
        self.barrier()
        for b in self.phase_bufs:
            if b.dsem is not None:
                self.free_dsems.append(b.dsem)
        self.stack.close()
        self.stack = None

    def sb(self, name, shape, dt=F32):
        self.uid = getattr(self, "uid", 0) + 1
        t = self.stack.enter_context(self.nc.sbuf_tensor(f"{name}_{self.uid}", list(shape), dt))
        n = 1
        for s in shape[1:]:
            n *= s
        self.sb_bytes += n * (2 if dt == BF16 else 4)
        b = Buf(t, name)
        self.phase_bufs.append(b)
        return b

    def ps(self, name, shape, dt=F32):
        self.uid = getattr(self, "uid", 0) + 1
        t = self.stack.enter_context(self.nc.psum_tensor(f"{name}_{self.uid}", list(shape), dt))
        b = Buf(t, name)
        b.psum = True
        self.phase_bufs.append(b)
        return b

    def _wait(self, e, deps, raw=()):
        for key, val in deps.items():
            if val <= 0 or self.seen[e].get(key, 0) >= val:
                continue
            if key.startswith("e" + e + "_") and key not in raw:
                continue
            self.eng[e].wait_ge(self.sems[key], val)
            self.seen[e][key] = val
            self.n_wait += 1

    @staticmethod
    def _merge(d, key, val):
        if d.get(key, 0) < val:
            d[key] = val

    def _deps(self, reads, writes, skip=None):
        deps = {}
        raw = set()
        for b in reads:
            for k, v in b.w.items():
                if k != skip:
                    self._merge(deps, k, v)
                    raw.add(k)
            if getattr(b, "psum", False):
                for k, v in b.r.items():
                    self._merge(deps, k, v)
        for b in writes:
            for k, v in b.w.items():
                if k != skip:
                    self._merge(deps, k, v)
            for k, v in b.r.items():
                self._merge(deps, k, v)
        self._raw = raw
        return deps

    def op(self, e, fn, reads=(), writes=()):
        if self.mute:
            return None
        deps = self._deps(reads, writes)
        self._wait(e, deps, self._raw if e != "pe" else ())
        if self.cnt[self.esem[e]] >= self.SEM_MAX:
            self.esem[e] = self.new_sem("e" + e)
        key = self.esem[e]
        ins = fn(self.eng[e])
        self.cnt[key] += 1
        val = self.cnt[key]
        ins.then_inc(self.sems[key], 1)
        self.n_ins += 1
        for b in reads:
            self._merge(b.r, key, val)
        for b in writes:
            b.w = {key: val}
            b.r = {}
        return ins

    def dma(self, out_ap, in_ap, sbuf, load, q="sp"):
        if self.mute:
            return None
        if sbuf.dsem is None or self.cnt[sbuf.dsem] + 16 >= self.SEM_MAX:
            sbuf.dsem = self._get_dsem()
        sem = sbuf.dsem
        if load:
            deps = self._deps([], [sbuf], skip=sem)
        else:
            deps = self._deps([sbuf], [])
        self._wait(q, deps)
        ins = self.eng[q].dma_start(out=out_ap, in_=in_ap)
        self.cnt[sem] += 16
        val = self.cnt[sem]
        ins.then_inc(self.sems[sem], 16)
        self.n_ins += 1
        if load:
            sbuf.w[sem] = val
            sbuf.r = {}
        else:
            self._merge(sbuf.r, sem, val)
        return ins

    def barrier(self):
        for e in self.eng:
            for key, val in self.cnt.items():
                if val > 0 and self.seen[e].get(key, 0) < val and not key.startswith("e" + e + "_"):
                    self.eng[e].wait_ge(self.sems[key], val)
                    self.seen[e][key] = val
                    self.n_wait += 1


def bc(ap, shape):
    return ap.to_broadcast(list(shape))


def row_bc(t, off, n):
    return bass.AP(t, off, [[0, 128], [1, n]])


def build(n_layers=DEPTH, dbg=False, skip=()):
    nc = bass.Bass("TRN2", target_bir_lowering=False)
    fw = FW(nc)
    fw.skip = tuple(skip)
    okind = "ExternalOutput" if dbg else "Internal"

    def din(name, shape):
        return nc.dram_tensor(name, list(shape), F32, kind="ExternalInput")

    x_in = din("x", [NTOK, D])
    norm_gain = din("norm_gain", [DEPTH, D])
    w_in = din("w_in", [DEPTH, D, NIN])
    w_vres_down = din("w_vres_down", [DEPTH - 1, D, 32])
    shift_prev = din("shift_prev", [DEPTH, SHW])
    shift_next = din("shift_next", [DEPTH, SHW])
    w_decay_up = din("w_decay_up", [DEPTH, 2, 64, D])
    decay_bias = din("decay_bias", [DEPTH, 2, D])
    w_iclr_up = din("w_iclr_up", [DEPTH, 2, 64, D])
    iclr_bias = din("iclr_bias", [DEPTH, 2, D])
    w_vres_up = din("w_vres_up", [DEPTH - 1, 32, D])
    vres_bias = din("vres_bias", [DEPTH - 1, D])
    k_k = din("k_k", [DEPTH, D])
    k_a = din("k_a", [DEPTH, D])
    r_k = din("r_k", [DEPTH, D])
    lnx_gain = din("lnx_gain", [DEPTH, D])
    lnx_bias = din("lnx_bias", [DEPTH, D])
    w_branch_a = din("w_branch_a", [DEPTH, D, D])
    ret_norm_gain = din("ret_norm_gain", [DEPTH, D])
    w_branch_b = din("w_branch_b", [DEPTH, D, D])
    w_out = din("w_out", [DEPTH, D, D])
    final_gain = din("final_gain", [1, D])
    out = nc.dram_tensor("out", [NTOK, D], F32, kind="ExternalOutput")

    def scr(name, shape):
        return nc.dram_tensor(name, list(shape), F32, kind=okind)

    proj = scr("proj", [NTOK, NINX])
    xs = [scr("xs0", [NTOK, D]), scr("xs1", [NTOK, D])]
    s_r = scr("s_r", [NTOK, D])
    s_a = scr("s_a", [NTOK, D])
    s_v = scr("s_v", [NTOK, D])
    s_w = [scr("s_w0", [NTOK, D]), scr("s_w1", [NTOK, D])]
    s_b = [scr("s_b0", [NTOK, D]), scr("s_b1", [NTOK, D])]
    s_k = [scr("s_k0", [NTOK, D]), scr("s_k1", [NTOK, D])]
    vfirst = scr("vfirst", [NTOK, D])
    bonus = scr("bonus", [NTOK, D])
    ysc = [scr("ysc0", [NTOK, D]), scr("ysc1", [NTOK, D])]
    yret = scr("yret", [NTOK, D])

    def scrb(name, shape, dt=BF16):
        return nc.dram_tensor(name, list(shape), dt, kind="Internal")

    fm_a = [scrb(f"fm_a{d}", [8, 128, NTOK]) for d in range(2)]
    fm_r = [scrb(f"fm_r{d}", [8, 128, NTOK]) for d in range(2)]
    fm_b = [scrb(f"fm_b{d}", [8, 128, NTOK]) for d in range(2)]
    fm_k = [scrb(f"fm_k{d}", [8, 128, NTOK]) for d in range(2)]
    tm_b = [scrb(f"tm_b{d}", [NTOK, D]) for d in range(2)]
    tm_k = [scrb(f"tm_k{d}", [NTOK, D]) for d in range(2)]
    tm_v = scrb("tm_v", [NTOK, D])
    elc = [scrb(f"elc{d}", [128, 512], F32) for d in range(2)]

    ident_t = nc.alloc_sbuf_tensor("ident", [128, 128], BF16)
    ident = Buf(ident_t, "ident")
    cos_t = Buf(nc.alloc_sbuf_tensor("cos_t", [128, 16, 32], F32), "cos")
    sin_t = Buf(nc.alloc_sbuf_tensor("sin_t", [128, 16, 32], F32), "sin")

    fw.begin_phase()
    io = fw.sb("io", [128, 128])
    fw.op("pool", lambda e: e.iota(io.t[:], pattern=[[1, 128]], base=0, channel_multiplier=-1,
                                   allow_small_or_imprecise_dtypes=True), writes=[io])
    fw.op("dve", lambda e: e.tensor_scalar(out=ident.t[:], in0=io.t[:], scalar1=0.0, scalar2=None,
                                           op0=ALU.is_equal), reads=[io], writes=[ident])
    pos = fw.sb("pos", [128, 16])
    fw.op("pool", lambda e: e.iota(pos.t[:], pattern=[[128, 16]], base=0, channel_multiplier=1,
                                   allow_small_or_imprecise_dtypes=True), writes=[pos])
    freq = fw.sb("freq", [128, 32])
    fr = np.power(np.float32(10000.0), -np.arange(32, dtype=np.float32) / np.float32(32)).astype(np.float32)
    for i in range(32):
        fw.op("pool", lambda e: e.memset(freq.t[:, i:i + 1], float(fr[i])), writes=[freq])
    ang = fw.sb("ang", [128, 16, 32])
    fw.op("dve", lambda e: e.tensor_tensor(out=ang.t[:], in0=bc(pos.t[:].unsqueeze(2), [128, 16, 32]),
                                           in1=bc(freq.t[:].unsqueeze(1), [128, 16, 32]), op=ALU.mult),
          reads=[pos, freq], writes=[ang])
    tmpa = fw.sb("tmpa", [128, 16, 32])
    negpi = fw.sb("negpi", [128, 1])
    fw.op("pool", lambda e: e.memset(negpi.t[:], -math.pi), writes=[negpi])
    tmpi = fw.sb("tmpi", [128, 16, 32], mybir.dt.int32)
    tmpf = fw.sb("tmpf", [128, 16, 32])
    for (shift, dst) in ((math.pi, sin_t), (1.5 * math.pi, cos_t)):
        fw.op("dve", lambda e: e.tensor_scalar(out=tmpa.t[:], in0=ang.t[:], scalar1=1.0 / (2 * math.pi),
                                               scalar2=shift / (2 * math.pi), op0=ALU.mult, op1=ALU.add),
              reads=[ang], writes=[tmpa])
        fw.op("dve", lambda e: e.tensor_copy(out=tmpi.t[:], in_=tmpa.t[:]), reads=[tmpa], writes=[tmpi])
        fw.op("dve", lambda e: e.tensor_copy(out=tmpf.t[:], in_=tmpi.t[:]), reads=[tmpi], writes=[tmpf])
        fw.op("dve", lambda e: e.tensor_tensor(out=tmpa.t[:], in0=tmpa.t[:], in1=tmpf.t[:], op=ALU.subtract),
              reads=[tmpa, tmpf], writes=[tmpa])
        fw.op("dve", lambda e: e.tensor_scalar(out=tmpf.t[:], in0=tmpa.t[:], scalar1=0.0, scalar2=None, op0=ALU.is_lt),
              reads=[tmpa], writes=[tmpf])
        fw.op("dve", lambda e: e.tensor_tensor(out=tmpa.t[:], in0=tmpa.t[:], in1=tmpf.t[:], op=ALU.add),
              reads=[tmpa, tmpf], writes=[tmpa])
        fw.op("act", lambda e: e.activation(out=dst.t[:], in_=tmpa.t[:], func=AF.Sin, scale=2 * math.pi,
                                            bias=negpi.t[:, 0:1]), reads=[tmpa, negpi], writes=[dst])
    fw.end_phase()

    evac_rr = [0]

    def evac(out_ap, in_ap, reads, writes):
        evac_rr[0] ^= 1
        if evac_rr[0]:
            fw.op("act", lambda e: e.copy(out=out_ap, in_=in_ap), reads=reads, writes=writes)
        else:
            fw.op("dve", lambda e: e.tensor_copy(out=out_ap, in_=in_ap), reads=reads, writes=writes)

    def rstd_from(ss, n, eps):
        fw.op("dve", lambda e: e.tensor_scalar(out=ss.t[:], in0=ss.t[:], scalar1=1.0 / n, scalar2=eps,
                                               op0=ALU.mult, op1=ALU.add), reads=[ss], writes=[ss])
        fw.op("act", lambda e: e.activation(out=ss.t[:], in_=ss.t[:], func=AF.Sqrt), reads=[ss], writes=[ss])
        fw.op("dve", lambda e: e.reciprocal(out=ss.t[:], in_=ss.t[:]), reads=[ss], writes=[ss])

    def transpose8(src_bf, dstT, ptr, n=8):
        for kc in range(n):
            fw.op("pe", lambda e: e.transpose(out=ptr.t[:, kc, :], in_=src_bf.t[:, kc * 128:(kc + 1) * 128],
                                              identity=ident.t[:]), reads=[src_bf, ident], writes=[ptr])
        evac(dstT, ptr.t[:, 0:n, :], [ptr], [])

    for l in range(n_layers):
        xcur = x_in if l == 0 else xs[(l - 1) % 2]
        xnext = xs[l % 2]
        ncol = NIN if l == 0 else NINX
        last = (l == DEPTH - 1)

        fw.begin_phase("P12")
        hnT = fw.sb("hnT", [128, 8, NTOK], BF16)
        gain = fw.sb("gain", [128, D])
        fw.dma(gain.t[:], row_bc(norm_gain, l * D, D), gain, True)
        xt = [fw.sb(f"xt{i}", [128, D]) for i in range(2)]
        junk = fw.sb("junk", [128, D])
        ssq = [fw.sb(f"ssq{i}", [128, 1]) for i in range(2)]
        hn = [fw.sb(f"hn{i}", [128, D], BF16) for i in range(2)]
        ptr = [fw.ps(f"ptr{i}", [128, 8, 128], BF16) for i in range(2)]
        for ti in range(NT):
            b = ti % 2
            fw.dma(xt[b].t[:], xcur.ap()[ti * 128:(ti + 1) * 128, :], xt[b], True)
            fw.op("pool", lambda e: e.memset(ssq[b].t[:], 0.0), writes=[ssq[b]])
            fw.op("act", lambda e: e.activation(out=junk.t[:], in_=xt[b].t[:], func=AF.Square,
                                                accum_out=ssq[b].t[:]), reads=[xt[b], ssq[b]], writes=[junk, ssq[b]])
            rstd_from(ssq[b], D, NORM_EPS)
            fw.op("dve", lambda e: e.scalar_tensor_tensor(out=hn[b].t[:], in0=xt[b].t[:], scalar=ssq[b].t[:, 0:1],
                                                          in1=gain.t[:], op0=ALU.mult, op1=ALU.mult),
                  reads=[xt[b], ssq[b], gain], writes=[hn[b]])
            for kc in range(8):
                fw.op("pe", lambda e: e.transpose(out=ptr[b].t[:, kc, :], in_=hn[b].t[:, kc * 128:(kc + 1) * 128],
                                                  identity=ident.t[:]), reads=[hn[b], ident], writes=[ptr[b]])
            evac(hnT.t[:, :, ti * 128:(ti + 1) * 128], ptr[b].t[:], [ptr[b]], [hnT])

        wst = [fw.sb(f"wst{i}", [128, 8, 512]) for i in range(2)]
        wbf = [fw.sb(f"wbf{i}", [128, 8, 512], BF16) for i in range(2)]
        pp = [fw.ps(f"pp{i}", [128, 512]) for i in range(4)]
        stg = [fw.sb(f"stg{i}", [128, 512]) for i in range(4)]
        ncb = (ncol + 511) // 512
        it = 0
        for cb in range(ncb):
            c0 = cb * 512
            cw = min(512, ncol - c0)
            wb = cb % 2
            cmain = min(cw, NIN - c0)
            fw.dma(wst[wb].t[:, :, 0:cmain],
                   bass.AP(w_in, l * D * NIN + c0, [[NIN, 128], [128 * NIN, 8], [1, cmain]]), wst[wb], True)
            if cw > cmain:
                fw.dma(wst[wb].t[:, :, cmain:cw],
                       bass.AP(w_vres_down, (l - 1) * D * 32, [[32, 128], [128 * 32, 8], [1, 32]]), wst[wb], True)
            fw.op("pool", lambda e: e.tensor_copy(out=wbf[wb].t[:, :, 0:cw], in_=wst[wb].t[:, :, 0:cw]),
                  reads=[wst[wb]], writes=[wbf[wb]])
            for ti in range(NT):
                pb = it % 4
                it += 1
                for kc in range(8):
                    fw.op("pe", lambda e: e.matmul(pp[pb].t[:, 0:cw], lhsT=hnT.t[:, kc, ti * 128:(ti + 1) * 128],
                                                   rhs=wbf[wb].t[:, kc, 0:cw], start=(kc == 0), stop=(kc == 7)),
                          reads=[hnT, wbf[wb]], writes=[pp[pb]])
                evac(stg[pb].t[:, 0:cw], pp[pb].t[:, 0:cw], [pp[pb]], [stg[pb]])
                fw.dma(proj.ap()[ti * 128:(ti + 1) * 128, c0:c0 + cw], stg[pb].t[:, 0:cw], stg[pb], False)
        fw.end_phase()

        fw.begin_phase("P3")
        mup = fw.sb("mup", [128, SHW])
        mun = fw.sb("mun", [128, SHW])
        fw.dma(mup.t[:], row_bc(shift_prev, l * SHW, SHW), mup, True)
        fw.dma(mun.t[:], row_bc(shift_next, l * SHW, SHW), mun, True)
        decb = [fw.sb(f"decb{d}", [128, D]) for d in range(2)]
        iclb = [fw.sb(f"iclb{d}", [128, D]) for d in range(2)]
        for d in range(2):
            fw.dma(decb[d].t[:], row_bc(decay_bias, (l * 2 + d) * D, D), decb[d], True)
            fw.dma(iclb[d].t[:], row_bc(iclr_bias, (l * 2 + d) * D, D), iclb[d], True)
        kkb = fw.sb("kkb", [128, D])
        kab = fw.sb("kab", [128, D])
        rkb = fw.sb("rkb", [128, D])
        fw.dma(kkb.t[:], row_bc(k_k, l * D, D), kkb, True)
        fw.dma(kab.t[:], row_bc(k_a, l * D, D), kab, True)
        fw.dma(rkb.t[:], row_bc(r_k, l * D, D), rkb, True)
        wstg = fw.sb("wstg", [128, D])
        wdec = fw.sb("wdec", [128, D], BF16)
        wicl = fw.sb("wicl", [128, D], BF16)
        fw.dma(wstg.t[:], bass.AP(w_decay_up, l * 128 * D, [[D, 128], [1, D]]), wstg, True)
        fw.op("dve", lambda e: e.tensor_copy(out=wdec.t[:], in_=wstg.t[:]), reads=[wstg], writes=[wdec])
        fw.dma(wstg.t[:], bass.AP(w_iclr_up, l * 128 * D, [[D, 128], [1, D]]), wstg, True)
        fw.op("dve", lambda e: e.tensor_copy(out=wicl.t[:], in_=wstg.t[:]), reads=[wstg], writes=[wicl])
        if l > 0:
            vrb = fw.sb("vrb", [128, D])
            fw.dma(vrb.t[:], row_bc(vres_bias, (l - 1) * D, D), vrb, True)
            wvr = fw.sb("wvr", [32, D], BF16)
            fw.dma(wstg.t[0:32, :], bass.AP(w_vres_up, (l - 1) * 32 * D, [[D, 32], [1, D]]), wstg, True)
            fw.op("dve", lambda e: e.tensor_copy(out=wvr.t[:], in_=wstg.t[0:32, :]), reads=[wstg], writes=[wvr])
            vl = fw.sb("vl", [128, 32])
            vf = fw.sb("vf", [128, D])
            vg = fw.sb("vg", [128, D])
            v_o = fw.sb("v_o", [128, D])
            pvr = fw.ps("pvr", [128, D])
        c0b = fw.sb("c0b", [128, SHW])
        cmb = fw.sb("cmb", [128, SHW])
        cpb = fw.sb("cpb", [128, SHW])
        cdbf = fw.sb("cdbf", [128, 384], BF16)
        cdT = fw.sb("cdT", [128, 384], BF16)
        pT = fw.ps("pT", [128, 1024], BF16)
        pdec = fw.ps("pdec", [128, D])
        picl = fw.ps("picl", [128, D])
        kk = fw.sb("kk", [128, D])
        tmp = fw.sb("tmp", [128, D])
        rk = fw.sb("rk", [128, D])
        a_o = fw.sb("a_o", [128, D])
        al = fw.sb("al", [128, D])
        xd = fw.sb("xd", [128, D])
        w_o = [fw.sb(f"w_o{d}", [128, D]) for d in range(2)]
        b_o = [fw.sb(f"b_o{d}", [128, D]) for d in range(2)]
        kd_o = [fw.sb(f"kd_o{d}", [128, D]) for d in range(2)]
        bon_o = fw.sb("bon_o", [128, D])
        ss16 = fw.sb("ss16", [128, 16])
        sd = [fw.sb(f"sd{d}", [128, 16]) for d in range(2)]
        for ti in range(NT):
            j = ti % 16
            t0 = ti * 128
            rows = slice(t0, t0 + 128)
            fw.dma(c0b.t[:], proj.ap()[rows, 0:SHW], c0b, True)
            if j == 0:
                fw.op("pool", lambda e: e.memset(cmb.t[:], 0.0), writes=[cmb])
                fw.dma(cmb.t[1:128, :], proj.ap()[t0:t0 + 127, 0:SHW], cmb, True)
            else:
                fw.dma(cmb.t[:], proj.ap()[t0 - 1:t0 + 127, 0:SHW], cmb, True)
            if j == 15:
                fw.op("pool", lambda e: e.memset(cpb.t[:], 0.0), writes=[cpb])
                fw.dma(cpb.t[0:127, :], proj.ap()[t0 + 1:t0 + 128, 0:SHW], cpb, True)
            else:
                fw.dma(cpb.t[:], proj.ap()[t0 + 1:t0 + 129, 0:SHW], cpb, True)
            if l > 0:
                fw.dma(vl.t[:], proj.ap()[rows, C_VL:C_VL + 32], vl, True)
                fw.dma(vf.t[:], vfirst.ap()[rows, :], vf, True)
            fw.op("dve", lambda e: e.tensor_tensor(out=cmb.t[:], in0=cmb.t[:], in1=c0b.t[:], op=ALU.subtract),
                  reads=[c0b, cmb], writes=[cmb])
            fw.op("dve", lambda e: e.tensor_tensor(out=cmb.t[:], in0=cmb.t[:], in1=mup.t[:], op=ALU.mult),
                  reads=[cmb, mup], writes=[cmb])
            fw.op("pool", lambda e: e.tensor_tensor(out=cpb.t[:], in0=cpb.t[:], in1=c0b.t[:], op=ALU.subtract),
                  reads=[c0b, cpb], writes=[cpb])
            fw.op("pool", lambda e: e.tensor_tensor(out=cpb.t[:], in0=cpb.t[:], in1=mun.t[:], op=ALU.mult),
                  reads=[cpb, mun], writes=[cpb])
            fw.op("dve", lambda e: e.tensor_tensor(out=c0b.t[:], in0=c0b.t[:], in1=cmb.t[:], op=ALU.add),
                  reads=[c0b, cmb], writes=[c0b])
            fw.op("dve", lambda e: e.tensor_tensor(out=c0b.t[:], in0=c0b.t[:], in1=cpb.t[:], op=ALU.add),
                  reads=[c0b, cpb], writes=[c0b])
            sh = c0b
            r_ap = sh.t[:, C_R:C_R + D]
            k_ap = sh.t[:, C_K:C_K + D]
            v_ap = sh.t[:, C_V:C_V + D]
            fw.op("act", lambda e: e.activation(out=cdbf.t[:, 0:128], in_=sh.t[:, C_CODE:C_CODE + 128], func=AF.Tanh),
                  reads=[sh], writes=[cdbf])
            fw.op("act", lambda e: e.copy(out=cdbf.t[:, 128:256], in_=sh.t[:, C_CODE + 128:C_CODE + 256]),
                  reads=[sh], writes=[cdbf])
            ncode = 2
            if l > 0:
                fw.op("act", lambda e: e.copy(out=cdbf.t[:, 256:288], in_=vl.t[:]), reads=[vl], writes=[cdbf])
            for c in range(2):
                fw.op("pe", lambda e: e.transpose(out=pT.t[:, c * 128:(c + 1) * 128], in_=cdbf.t[:, c * 128:(c + 1) * 128],
                                                  identity=ident.t[:]), reads=[cdbf, ident], writes=[pT])
            if l > 0:
                fw.op("pe", lambda e: e.transpose(out=pT.t[0:32, 256:384], in_=cdbf.t[:, 256:288],
                                                  identity=ident.t[:]), reads=[cdbf, ident], writes=[pT])
                fw.op("act", lambda e: e.copy(out=cdT.t[0:32, 256:384], in_=pT.t[0:32, 256:384]), reads=[pT], writes=[cdT])
            fw.op("act", lambda e: e.copy(out=cdT.t[:, 0:256], in_=pT.t[:, 0:256]), reads=[pT], writes=[cdT])
            fw.op("dve", lambda e: e.tensor_tensor(out=kk.t[:], in0=k_ap, in1=kkb.t[:], op=ALU.mult),
                  reads=[sh, kkb], writes=[kk])
            fw.op("dve", lambda e: e.tensor_tensor(out=tmp.t[:], in0=kk.t[:], in1=kk.t[:], op=ALU.mult),
                  reads=[kk], writes=[tmp])
            fw.op("dve", lambda e: e.tensor_reduce(out=ss16.t[:], in_=tmp.t[:].rearrange("p (h k) -> p h k", h=16),
                                                   axis=AX.X, op=ALU.add), reads=[tmp], writes=[ss16])
            fw.op("dve", lambda e: e.tensor_scalar(out=ss16.t[:], in0=ss16.t[:], scalar1=1e-12, scalar2=None,
                                                   op0=ALU.add), reads=[ss16], writes=[ss16])
            fw.op("act", lambda e: e.activation(out=ss16.t[:], in_=ss16.t[:], func=AF.Sqrt), reads=[ss16], writes=[ss16])
            fw.op("dve", lambda e: e.reciprocal(out=ss16.t[:], in_=ss16.t[:]), reads=[ss16], writes=[ss16])
            fw.op("dve", lambda e: e.tensor_scalar(out=ss16.t[:], in0=ss16.t[:], scalar1=-1.0, scalar2=None,
                                                   op0=ALU.mult), reads=[ss16], writes=[ss16])
            fw.op("dve", lambda e: e.tensor_tensor(
                out=a_o.t[:].rearrange("p (h k) -> p h k", h=16), in0=kk.t[:].rearrange("p (h k) -> p h k", h=16),
                in1=bc(ss16.t[:].unsqueeze(2), [128, 16, 64]), op=ALU.mult),
                reads=[kk, ss16], writes=[a_o])
            fw.op("dve", lambda e: e.tensor_tensor(out=rk.t[:], in0=r_ap, in1=rkb.t[:], op=ALU.mult),
                  reads=[sh, rkb], writes=[rk])
            if l > 0:
                for hb in range(2):
                    fw.op("pe", lambda e: e.matmul(pvr.t[:, hb * 512:(hb + 1) * 512], lhsT=cdT.t[0:32, 256:384],
                                                   rhs=wvr.t[0:32, hb * 512:(hb + 1) * 512], start=True, stop=True),
                          reads=[cdT, wvr], writes=[pvr])
                fw.op("dve", lambda e: e.tensor_tensor(out=vg.t[:], in0=pvr.t[:], in1=vrb.t[:], op=ALU.add),
                      reads=[pvr, vrb], writes=[vg])
                fw.op("act", lambda e: e.activation(out=vg.t[:], in_=vg.t[:], func=AF.Sigmoid), reads=[vg], writes=[vg])
                fw.op("pool", lambda e: e.tensor_tensor(out=vf.t[:], in0=vf.t[:], in1=v_ap, op=ALU.subtract),
                      reads=[vf, sh], writes=[vf])
                fw.op("pool", lambda e: e.tensor_tensor(out=vf.t[:], in0=vf.t[:], in1=vg.t[:], op=ALU.mult),
                      reads=[vf, vg], writes=[vf])
                fw.op("pool", lambda e: e.tensor_tensor(out=v_o.t[:], in0=vf.t[:], in1=v_ap, op=ALU.add),
                      reads=[vf, sh], writes=[v_o])
                vbuf, vap = v_o, v_o.t[:]
            else:
                vbuf, vap = sh, v_ap
            for d in range(2):
                for hb in range(2):
                    cs = slice(hb * 512, (hb + 1) * 512)
                    fw.op("pe", lambda e: e.matmul(pdec.t[:, cs], lhsT=cdT.t[d * 64:(d + 1) * 64, 0:128],
                                                   rhs=wdec.t[d * 64:(d + 1) * 64, cs], start=True, stop=True),
                          reads=[cdT, wdec], writes=[pdec])
                    fw.op("pe", lambda e: e.matmul(picl.t[:, cs], lhsT=cdT.t[d * 64:(d + 1) * 64, 128:256],
                                                   rhs=wicl.t[d * 64:(d + 1) * 64, cs], start=True, stop=True),
                          reads=[cdT, wicl], writes=[picl])
                fw.op("dve", lambda e: e.tensor_tensor(out=xd.t[:], in0=pdec.t[:], in1=decb[d].t[:], op=ALU.add),
                      reads=[pdec, decb[d]], writes=[xd])
                fw.op("act", lambda e: e.activation(out=xd.t[:], in_=xd.t[:], func=AF.Sigmoid), reads=[xd], writes=[xd])
                fw.op("act", lambda e: e.mul(out=w_o[d].t[:], in_=xd.t[:], mul=-math.exp(-0.5)),
                      reads=[xd], writes=[w_o[d]])
                fw.op("dve", lambda e: e.tensor_tensor(out=al.t[:], in0=picl.t[:], in1=iclb[d].t[:], op=ALU.add),
                      reads=[picl, iclb[d]], writes=[al])
                fw.op("act", lambda e: e.activation(out=al.t[:], in_=al.t[:], func=AF.Sigmoid), reads=[al], writes=[al])
                fw.op("dve", lambda e: e.scalar_tensor_tensor(out=b_o[d].t[:], in0=a_o.t[:], scalar=-1.0, in1=al.t[:],
                                                               op0=ALU.mult, op1=ALU.mult),
                      reads=[a_o, al], writes=[b_o[d]])
                fw.op("dve", lambda e: e.scalar_tensor_tensor(out=tmp.t[:], in0=al.t[:], scalar=-1.0, in1=kab.t[:],
                                                              op0=ALU.add, op1=ALU.mult),
                      reads=[al, kab], writes=[tmp])
                fw.op("dve", lambda e: e.scalar_tensor_tensor(out=kd_o[d].t[:], in0=tmp.t[:], scalar=1.0, in1=k_ap,
                                                              op0=ALU.add, op1=ALU.mult),
                      reads=[tmp, sh], writes=[kd_o[d]])
                fw.op("dve", lambda e: e.tensor_tensor(out=tmp.t[:], in0=rk.t[:], in1=kd_o[d].t[:], op=ALU.mult),
                      reads=[rk, kd_o[d]], writes=[tmp])
                fw.op("dve", lambda e: e.tensor_reduce(out=sd[d].t[:], in_=tmp.t[:].rearrange("p (h k) -> p h k", h=16),
                                                       axis=AX.X, op=ALU.add), reads=[tmp], writes=[sd[d]])
                fw.dma(s_w[d].ap()[rows, :], w_o[d].t[:], w_o[d], False)
                fw.dma(s_b[d].ap()[rows, :], b_o[d].t[:], b_o[d], False)
                fw.dma(s_k[d].ap()[rows, :], kd_o[d].t[:], kd_o[d], False)
            fw.op("dve", lambda e: e.tensor_tensor(out=sd[0].t[:], in0=sd[0].t[:], in1=sd[1].t[:], op=ALU.add),
                  reads=[sd[0], sd[1]], writes=[sd[0]])
            fw.op("dve", lambda e: e.tensor_tensor(
                out=bon_o.t[:].rearrange("p (h k) -> p h k", h=16), in0=vap.rearrange("p (h k) -> p h k", h=16),
                in1=bc(sd[0].t[:].unsqueeze(2), [128, 16, 64]), op=ALU.mult), reads=[vbuf, sd[0]], writes=[bon_o])
            fw.dma(bonus.ap()[rows, :], bon_o.t[:], bon_o, False)
            fw.dma(s_r.ap()[rows, :], r_ap, sh, False)
            fw.dma(s_a.ap()[rows, :], a_o.t[:], a_o, False)
            fw.dma(s_v.ap()[rows, :], vap, vbuf, False)
            if l == 0:
                fw.dma(vfirst.ap()[rows, :], vap, vbuf, False)
        fw.end_phase()

        fw.begin_phase("P3b")
        dif = fw.sb("dif", [128, 128])
        fw.op("pool", lambda e: e.iota(dif.t[:], pattern=[[1, 128]], base=0, channel_multiplier=-1,
                                       allow_small_or_imprecise_dtypes=True), writes=[dif])
        pcol = fw.sb("pcol", [128, 128])
        fw.op("pool", lambda e: e.iota(pcol.t[:], pattern=[[0, 128]], base=0, channel_multiplier=1,
                                       allow_small_or_imprecise_dtypes=True), writes=[pcol])
        jrow = fw.sb("jrow", [128, 128])
        fw.op("pool", lambda e: e.iota(jrow.t[:], pattern=[[1, 128]], base=0, channel_multiplier=0,
                                       allow_small_or_imprecise_dtypes=True), writes=[jrow])
        fw.op("dve", lambda e: e.tensor_scalar(out=pcol.t[:], in0=pcol.t[:], scalar1=64.0, scalar2=None, op0=ALU.is_ge),
              reads=[pcol], writes=[pcol])
        fw.op("dve", lambda e: e.tensor_scalar(out=jrow.t[:], in0=jrow.t[:], scalar1=64.0, scalar2=None, op0=ALU.is_ge),
              reads=[jrow], writes=[jrow])
        same = fw.sb("same", [128, 128])
        fw.op("dve", lambda e: e.tensor_tensor(out=same.t[:], in0=pcol.t[:], in1=jrow.t[:], op=ALU.is_equal),
              reads=[pcol, jrow], writes=[same])
        tri = {}
        for nm, cmp in (("A", ALU.is_ge), ("B", ALU.is_lt), ("C", ALU.is_le), ("D", ALU.is_gt)):
            tri[nm] = fw.sb("tri" + nm, [128, 128])
            fw.op("dve", lambda e: e.tensor_scalar(out=tri[nm].t[:], in0=dif.t[:], scalar1=0.0, scalar2=None, op0=cmp),
                  reads=[dif], writes=[tri[nm]])
            fw.op("dve", lambda e: e.tensor_tensor(out=tri[nm].t[:], in0=tri[nm].t[:], in1=same.t[:], op=ALU.mult),
                  reads=[tri[nm], same], writes=[tri[nm]])
        ind = fw.sb("ind", [128, 2])
        fw.op("dve", lambda e: e.tensor_copy(out=ind.t[:, 1:2], in_=pcol.t[:, 0:1]), reads=[pcol], writes=[ind])
        fw.op("dve", lambda e: e.tensor_scalar(out=ind.t[:, 0:1], in0=pcol.t[:, 0:1], scalar1=-1.0, scalar2=1.0,
                                               op0=ALU.mult, op1=ALU.add), reads=[pcol], writes=[ind])
        elcs = [fw.sb(f"elcs{d}", [128, 8, 64]) for d in range(2)]
        LD = {nm: [fw.sb(f"ld{nm}{i}", [128, D]) for i in range(2)] for nm in ("r", "a", "v")}
        LDd = {nm: [fw.sb(f"ld{nm}{i}", [128, D]) for i in range(2)] for nm in ("w", "b", "k")}
        pLin = fw.ps("pLin", [128, D])
        pLsf = fw.ps("pLsf", [128, D])
        pel = fw.ps("pel", [128, 512])
        ptr3 = [fw.ps(f"ptr3{i}", [128, 8, 128], BF16) for i in range(2)]
        eLin = fw.sb("eLin", [128, D])
        emL = fw.sb("emL", [128, D])
        eLx = fw.sb("eLx", [128, D])
        eLsf = fw.sb("eLsf", [128, D])
        OB = {nm: fw.sb("ob" + nm, [128, D], BF16) for nm in ("Ab", "Rb", "Bh", "Kh", "Bt", "Kt", "Vb")}
        XT = [fw.sb(f"XT{i}", [128, 8, 128], BF16) for i in range(2)]
        assert fw.sb_bytes < 190 * 1024, fw.sb_bytes
        xcnt = 0
        for ti in range(NT):
            rows = slice(ti * 128, (ti + 1) * 128)
            pb_ = ti % 2
            r_t, a_t, v_t = LD["r"][pb_], LD["a"][pb_], LD["v"][pb_]
            fw.dma(r_t.t[:], s_r.ap()[rows, :], r_t, True)
            fw.dma(a_t.t[:], s_a.ap()[rows, :], a_t, True)
            fw.dma(v_t.t[:], s_v.ap()[rows, :], v_t, True)
            fw.op("act", lambda e: e.copy(out=OB["Vb"].t[:], in_=v_t.t[:]), reads=[v_t], writes=[OB["Vb"]])
            fw.dma(tm_v.ap()[rows, :], OB["Vb"].t[:], OB["Vb"], False)
            for d in range(2):
                w_t, b_t, k_t = LDd["w"][d], LDd["b"][d], LDd["k"][d]
                fw.dma(w_t.t[:], s_w[d].ap()[rows, :], w_t, True)
                fw.dma(b_t.t[:], s_b[d].ap()[rows, :], b_t, True)
                fw.dma(k_t.t[:], s_k[d].ap()[rows, :], k_t, True)
                tin, tsf = (tri["A"], tri["B"]) if d == 0 else (tri["C"], tri["D"])
                for hb in range(2):
                    cs = slice(hb * 512, (hb + 1) * 512)
                    fw.op("pe", lambda e: e.matmul(pLin.t[:, cs], lhsT=tin.t[:], rhs=w_t.t[:, cs], start=True, stop=True),
                          reads=[tin, w_t], writes=[pLin])
                    fw.op("pe", lambda e: e.matmul(pLsf.t[:, cs], lhsT=tsf.t[:], rhs=w_t.t[:, cs], start=True, stop=True),
                          reads=[tsf, w_t], writes=[pLsf])
                for fc in range(8):
                    fw.op("pe", lambda e: e.matmul(pel.t[:, fc * 2:fc * 2 + 2], lhsT=w_t.t[:, fc * 128:(fc + 1) * 128],
                                                   rhs=ind.t[:], start=(fc == 0), stop=(fc == 7), skip_group_check=True),
                          reads=[w_t, ind], writes=[pel])
                fw.op("act", lambda e: e.activation(out=elcs[d].t[:, :, 2 * ti:2 * ti + 2],
                                                    in_=pel.t[:, 0:16].rearrange("p (f c) -> p f c", c=2), func=AF.Exp),
                      reads=[pel], writes=[elcs[d]])
                fw.op("act", lambda e: e.activation(out=eLin.t[:], in_=pLin.t[:], func=AF.Exp), reads=[pLin], writes=[eLin])
                fw.op("act", lambda e: e.activation(out=emL.t[:], in_=pLin.t[:], func=AF.Exp, scale=-1.0),
                      reads=[pLin], writes=[emL])
                fw.op("dve", lambda e: e.tensor_tensor(out=eLx.t[:], in0=pLin.t[:], in1=w_t.t[:], op=ALU.subtract),
                      reads=[pLin, w_t], writes=[eLx])
                fw.op("act", lambda e: e.activation(out=eLx.t[:], in_=eLx.t[:], func=AF.Exp), reads=[eLx], writes=[eLx])
                fw.op("act", lambda e: e.activation(out=eLsf.t[:], in_=pLsf.t[:], func=AF.Exp), reads=[pLsf], writes=[eLsf])
                for (onm, x_, e_) in (("Ab", a_t, eLx), ("Rb", r_t, eLin), ("Bh", b_t, emL), ("Kh", k_t, emL),
                                      ("Bt", b_t, eLsf), ("Kt", k_t, eLsf)):
                    eng = "dve" if onm in ("Ab", "Rb", "Bh", "Kh") else "pool"
                    fw.op(eng, lambda e: e.tensor_tensor(out=OB[onm].t[:], in0=x_.t[:], in1=e_.t[:], op=ALU.mult),
                          reads=[x_, e_], writes=[OB[onm]])
                fw.dma(tm_b[d].ap()[rows, :], OB["Bt"].t[:], OB["Bt"], False)
                fw.dma(tm_k[d].ap()[rows, :], OB["Kt"].t[:], OB["Kt"], False)
                for (onm, dst) in (("Ab", fm_a[d]), ("Rb", fm_r[d]), ("Bh", fm_b[d]), ("Kh", fm_k[d])):
                    pt = ptr3[xcnt % 2]
                    xt_ = XT[xcnt % 2]
                    xcnt += 1
                    for kc in range(8):
                        fw.op("pe", lambda e: e.transpose(out=pt.t[:, kc, :], in_=OB[onm].t[:, kc * 128:(kc + 1) * 128],
                                                          identity=ident.t[:]), reads=[OB[onm], ident], writes=[pt])
                    fw.op("act", lambda e: e.copy(out=xt_.t[:], in_=pt.t[:]), reads=[pt], writes=[xt_])
                    fw.dma(bass.AP(dst, ti * 128, [[NTOK, 128], [128 * NTOK, 8], [1, 128]]), xt_.t[:], xt_, False)
        for d in range(2):
            fw.dma(elc[d].ap(), elcs[d].t[:].rearrange("p f c -> p (f c)"), elcs[d], False)
        fw.end_phase()

        fw.begin_phase("P4")
        CH = 64
        NCH = SEQ // CH
        dif64 = fw.sb("dif64", [64, 64])
        fw.op("pool", lambda e: e.iota(dif64.t[:], pattern=[[1, 64]], base=0, channel_multiplier=-1,
                                       allow_small_or_imprecise_dtypes=True), writes=[dif64])
        mk = {}
        for nm, cmp in (("LT", ALU.is_gt), ("LE", ALU.is_ge), ("GT", ALU.is_lt), ("GE", ALU.is_le)):
            mk[nm] = fw.sb("mk" + nm, [64, 64])
            fw.op("dve", lambda e: e.tensor_scalar(out=mk[nm].t[:], in0=dif64.t[:], scalar1=0.0, scalar2=None, op0=cmp),
                  reads=[dif64], writes=[mk[nm]])
        I2 = fw.sb("I2", [64, 2, 64])
        for h in range(2):
            fw.op("dve", lambda e: e.tensor_scalar(out=I2.t[:, h, :], in0=dif64.t[:], scalar1=0.0, scalar2=None,
                                                   op0=ALU.is_equal), reads=[dif64], writes=[I2])
        mbc = [fw.sb(f"mbc{d}", [64, 128]) for d in range(2)]
        fw.op("dve", lambda e: e.tensor_copy(out=mbc[0].t[:, 0:64], in_=mk["LT"].t[:]), reads=[mk["LT"]], writes=[mbc[0]])
        fw.op("dve", lambda e: e.tensor_copy(out=mbc[0].t[:, 64:128], in_=mk["LE"].t[:]), reads=[mk["LE"]], writes=[mbc[0]])
        fw.op("dve", lambda e: e.tensor_copy(out=mbc[1].t[:, 0:64], in_=mk["GT"].t[:]), reads=[mk["GT"]], writes=[mbc[1]])
        fw.op("dve", lambda e: e.tensor_copy(out=mbc[1].t[:, 64:128], in_=mk["GE"].t[:]), reads=[mk["GE"]], writes=[mbc[1]])
        mka = [mk["GT"], mk["LT"]]

        AR = [fw.sb(f"AR{i}", [64, 2, NCH, 2, CH], BF16) for i in range(2)]
        BhT = [fw.sb("BhT0", [64, 2, SEQ], BF16)] * 2
        KhT = [fw.sb("KhT0", [64, 2, SEQ], BF16)] * 2
        Bt_ = [fw.sb(f"Btm{i}", [64, NCH, 128], BF16) for i in range(2)]
        Kt_ = [fw.sb(f"Ktm{i}", [64, NCH, 128], BF16) for i in range(2)]
        Vt_ = [fw.sb(f"Vtm{i}", [64, NCH, 128], BF16) for i in range(2)]
        EL = [fw.sb(f"EL{i}", [64, 2, NCH]) for i in range(2)]
        TT = [fw.sb(f"TT{i}", [64, NCH, 2, CH], BF16) for i in range(2)]
        AKRK = [fw.sb(f"AKRK{i}", [64, NCH, 2, 2 * CH], BF16) for i in range(2)]
        ARB = [fw.sb(f"ARB{i}", [64, NCH, 2, CH], BF16) for i in range(2)]
        NLANE = 4
        NwL = [[fw.sb(f"Nw{ln}{i}", [64, 2, CH]) for i in range(2)] for ln in range(NLANE)]
        MQL = [[fw.sb(f"MQ{ln}{i}", [64, 2, 2 * CH]) for i in range(2)] for ln in range(NLANE)]
        Gb = fw.sb("Gb", [64, 128], BF16)
        Ub = fw.sb("Ub", [64, 128], BF16)
        Sf = fw.sb("Sf", [64, 2, 64])
        Sbf = fw.sb("Sbf", [64, 2, 64], BF16)
        Yst = [fw.sb(f"Yst{i}", [64, 8, 128]) for i in range(2)]
        ps_laneL = [fw.ps(f"ps_lane{ln}", [128, 512]) for ln in range(NLANE)]
        ps_gu = fw.ps("ps_gu", [128, 512])
        ps_ys = fw.ps("ps_ys", [128, 512])
        assert fw.sb_bytes < 186 * 1024, fw.sb_bytes

        scans = [(s, d, hp) for s in range(NSEQ) for d in range(2) for hp in range(8)]

        def load_scan(idx):
            s, d, hp = scans[idx]
            pb_ = idx % 2
            tok0 = s * SEQ
            for h in range(2):
                row0 = (hp * 128 + h * 64) * NTOK + tok0
                for which, src in ((0, fm_a[d]), (1, fm_r[d])):
                    for q4 in range(4):
                        fw.dma(AR[pb_].t[:, h, q4 * 8:(q4 + 1) * 8, which, :],
                               bass.AP(src, row0 + q4 * 8 * CH, [[NTOK, 64], [CH, 8], [1, CH]]), AR[pb_], True)
                fw.dma(BhT[pb_].t[:, h, :], bass.AP(fm_b[d], row0, [[NTOK, 64], [1, SEQ]]), BhT[pb_], True)
                fw.dma(KhT[pb_].t[:, h, :], bass.AP(fm_k[d], row0, [[NTOK, 64], [1, SEQ]]), KhT[pb_], True)
                fw.dma(EL[pb_].t[:, h, :], bass.AP(elc[d], h * 64 * 512 + hp * 64 + s * NCH, [[512, 64], [1, NCH]]),
                       EL[pb_], True)
            for dst, src in ((Bt_[pb_], tm_b[d]), (Kt_[pb_], tm_k[d]), (Vt_[pb_], tm_v)):
                for q4 in range(4):
                    fw.dma(dst.t[:, q4 * 8:(q4 + 1) * 8, :],
                           bass.AP(src, (tok0 + q4 * 8 * CH) * D + hp * 128, [[D, CH], [CH * D, 8], [1, 128]]), dst, True)

        def gen_A(idx, c, lane):
            Nw, MQ = NwL[lane], MQL[lane]
            ps_sc = ps_c = ps_inv = ps_laneL[lane]
            s, d, hp = scans[idx]
            pb_ = idx % 2
            t0 = c * CH
            ar = AR[pb_]
            for h in range(2):
                ar_h = ar.t[:, h, c, :, :].rearrange("p a b -> p (a b)")
                fw.op("pe", lambda e: e.matmul(ps_sc.t[0:64, h * 64:(h + 1) * 64], lhsT=ar.t[:, h, c, 0, :],
                                               rhs=BhT[pb_].t[:, h, t0:t0 + CH], start=(h == 0), stop=False,
                                               skip_group_check=True), reads=[ar, BhT[pb_]], writes=[ps_sc])
                fw.op("pe", lambda e: e.matmul(ps_sc.t[0:64, 128 + h * 128:128 + (h + 1) * 128],
                                               lhsT=BhT[pb_].t[:, h, t0:t0 + CH], rhs=ar_h, start=False, stop=(h == 1),
                                               skip_group_check=True), reads=[ar, BhT[pb_]], writes=[ps_sc])
            yield
            n0, mq0 = Nw[0], MQ[0]
            sc_n = ps_sc.t[0:64, 0:128].rearrange("p (h x) -> p h x", h=2)
            sc_b = ps_sc.t[0:64, 128:384].rearrange("p (h x) -> p h x", h=2)
            sc_c = ps_c.t[0:64, 0:256].rearrange("p (h x) -> p h x", h=2)
            fw.op("dve", lambda e: e.tensor_tensor(out=n0.t[:], in0=sc_n, in1=bc(mka[d].t[:].unsqueeze(1), [64, 2, 64]),
                                                   op=ALU.mult), reads=[ps_sc, mka[d]], writes=[n0])
            fw.op("dve", lambda e: e.tensor_tensor(out=mq0.t[:, :, 0:64], in0=sc_b[:, :, 0:64],
                                                   in1=bc(mbc[d].t[:, 0:64].unsqueeze(1), [64, 2, 64]), op=ALU.mult),
                  reads=[ps_sc, mbc[d]], writes=[mq0])
            fw.op("dve", lambda e: e.tensor_tensor(out=ARB[pb_].t[:, c, :, :], in0=sc_b[:, :, 64:128],
                                                   in1=bc(mbc[d].t[:, 64:128].unsqueeze(1), [64, 2, 64]), op=ALU.mult),
                  reads=[ps_sc, mbc[d]], writes=[ARB[pb_]])
            yield
            for h in range(2):
                ar_h = ar.t[:, h, c, :, :].rearrange("p a b -> p (a b)")
                fw.op("pe", lambda e: e.matmul(ps_c.t[0:64, h * 128:(h + 1) * 128], lhsT=KhT[pb_].t[:, h, t0:t0 + CH],
                                               rhs=ar_h, start=(h == 0), stop=(h == 1), skip_group_check=True),
                      reads=[ar, KhT[pb_]], writes=[ps_c])
            yield
            fw.op("dve", lambda e: e.tensor_tensor(out=AKRK[pb_].t[:, c, :, :], in0=sc_c,
                                                   in1=bc(mbc[d].t[:].unsqueeze(1), [64, 2, 128]), op=ALU.mult),
                  reads=[ps_c, mbc[d]], writes=[AKRK[pb_]])
            fw.op("pool", lambda e: e.tensor_copy(out=mq0.t[:, :, 64:128], in_=I2.t[:]), reads=[I2], writes=[mq0])
            yield
            inv = ps_inv.t[0:64, 0:384].rearrange("p (h x) -> p h x", h=2)
            for j in range(5):
                nj, mqj = Nw[j % 2], MQ[j % 2]
                nn, mqn = Nw[(j + 1) % 2], MQ[(j + 1) % 2]
                for h in range(2):
                    fw.op("pe", lambda e: e.matmul(inv[:, h, 0:128], lhsT=nj.t[:, h, :], rhs=mqj.t[:, h, :],
                                                   start=(h == 0), stop=False, skip_group_check=True),
                          reads=[nj, mqj], writes=[ps_inv])
                    fw.op("pe", lambda e: e.matmul(inv[:, h, 128:192], lhsT=mqj.t[:, h, 0:64], rhs=nj.t[:, h, :],
                                                   start=False, stop=(h == 1), skip_group_check=True),
                          reads=[nj, mqj], writes=[ps_inv])
                yield
                if j < 4:
                    fw.op("act", lambda e: e.copy(out=mqn.t[:, :, 0:64], in_=inv[:, :, 0:64]), reads=[ps_inv], writes=[mqn])
                fw.op("dve", lambda e: e.tensor_tensor(out=mqn.t[:, :, 64:128], in0=inv[:, :, 64:128],
                                                       in1=mqj.t[:, :, 64:128], op=ALU.add),
                      reads=[ps_inv, mqj], writes=[mqn])
                fw.op("act", lambda e: e.copy(out=nn.t[:], in_=inv[:, :, 128:192]), reads=[ps_inv], writes=[nn])
                yield
            n5, mq5 = Nw[1], MQ[1]
            for h in range(2):
                fw.op("pe", lambda e: e.matmul(inv[:, h, 0:64], lhsT=n5.t[:, h, :], rhs=mq5.t[:, h, 64:128],
                                               start=(h == 0), stop=(h == 1), skip_group_check=True),
                      reads=[n5, mq5], writes=[ps_inv])
            yield
            fw.op("dve", lambda e: e.tensor_tensor(out=TT[pb_].t[:, c, :, :], in0=inv[:, :, 0:64], in1=mq5.t[:, :, 64:128],
                                                   op=ALU.add), reads=[ps_inv, mq5], writes=[TT[pb_]])

        def gen_B(idx, k):
            s, d, hp = scans[idx]
            pb_ = idx % 2
            c = k if d == 0 else NCH - 1 - k
            ar, vt, bt, kt = AR[pb_], Vt_[pb_], Bt_[pb_], Kt_[pb_]
            if k == 0:
                fw.op("pool", lambda e: e.memset(Sf.t[:], 0.0), writes=[Sf])
                fw.op("pool", lambda e: e.memset(Sbf.t[:], 0.0), writes=[Sbf])
            g_ = ps_gu.t[0:64, 0:128]
            for h in range(2):
                cs = slice(h * 64, (h + 1) * 64)
                fw.op("pe", lambda e: e.matmul(ps_gu.t[0:64, cs], lhsT=ar.t[:, h, c, 0, :], rhs=Sbf.t[:, h, :],
                                               start=(h == 0), stop=False, skip_group_check=True),
                      reads=[ar, Sbf], writes=[ps_gu])
                fw.op("pe", lambda e: e.matmul(ps_gu.t[0:64, cs], lhsT=AKRK[pb_].t[:, c, h, 0:64], rhs=vt.t[:, c, cs],
                                               start=False, stop=(h == 1), skip_group_check=True),
                      reads=[AKRK[pb_], vt], writes=[ps_gu])
            yield
            fw.op("act", lambda e: e.copy(out=Gb.t[:], in_=g_), reads=[ps_gu], writes=[Gb])
            yield
            for h in range(2):
                cs = slice(h * 64, (h + 1) * 64)
                fw.op("pe", lambda e: e.matmul(ps_gu.t[0:64, 128 + h * 64:128 + (h + 1) * 64], lhsT=TT[pb_].t[:, c, h, :], rhs=Gb.t[:, cs],
                                               start=(h == 0), stop=(h == 1), skip_group_check=True),
                      reads=[TT[pb_], Gb], writes=[ps_gu])
            yield
            fw.op("dve", lambda e: e.tensor_copy(out=Ub.t[:], in_=ps_gu.t[0:64, 128:256]), reads=[ps_gu], writes=[Ub])
            yield
            y_ = ps_ys.t[0:64, 0:128]
            for h in range(2):
                cs = slice(h * 64, (h + 1) * 64)
                fw.op("pe", lambda e: e.matmul(ps_ys.t[0:64, cs], lhsT=ar.t[:, h, c, 1, :], rhs=Sbf.t[:, h, :],
                                               start=(h == 0), stop=False, skip_group_check=True),
                      reads=[ar, Sbf], writes=[ps_ys])
                fw.op("pe", lambda e: e.matmul(ps_ys.t[0:64, cs], lhsT=ARB[pb_].t[:, c, h, :], rhs=Ub.t[:, cs],
                                               start=False, stop=False, skip_group_check=True),
                      reads=[ARB[pb_], Ub], writes=[ps_ys])
                fw.op("pe", lambda e: e.matmul(ps_ys.t[0:64, cs], lhsT=AKRK[pb_].t[:, c, h, 64:128], rhs=vt.t[:, c, cs],
                                               start=False, stop=(h == 1), skip_group_check=True),
                      reads=[AKRK[pb_], vt], writes=[ps_ys])
            yst = Yst[(k // 8) % 2]
            for h in range(2):
                cs = slice(h * 64, (h + 1) * 64)
                fw.op("pe", lambda e: e.matmul(ps_ys.t[0:64, 128 + h * 64:128 + (h + 1) * 64], lhsT=kt.t[:, c, cs], rhs=vt.t[:, c, cs],
                                               start=(h == 0), stop=False, skip_group_check=True),
                      reads=[kt, vt], writes=[ps_ys])
                fw.op("pe", lambda e: e.matmul(ps_ys.t[0:64, 128 + h * 64:128 + (h + 1) * 64], lhsT=bt.t[:, c, cs], rhs=Ub.t[:, cs],
                                               start=False, stop=(h == 1), skip_group_check=True),
                      reads=[bt, Ub], writes=[ps_ys])
            yield
            fw.op("act", lambda e: e.copy(out=yst.t[:, c % 8, :], in_=y_), reads=[ps_ys], writes=[yst])
            for h in range(2):
                fw.op("dve", lambda e: e.scalar_tensor_tensor(out=Sbf.t[:, h, :], in0=Sf.t[:, h, :],
                                                              scalar=EL[pb_].t[:, h, c:c + 1], in1=ps_ys.t[0:64, 128 + h * 64:128 + (h + 1) * 64],
                                                              op0=ALU.mult, op1=ALU.add),
                      reads=[Sf, EL[pb_], ps_ys], writes=[Sbf])
            for h in range(2):
                fw.op("dve", lambda e: e.scalar_tensor_tensor(out=Sf.t[:, h, :], in0=Sf.t[:, h, :],
                                                              scalar=EL[pb_].t[:, h, c:c + 1], in1=ps_ys.t[0:64, 128 + h * 64:128 + (h + 1) * 64],
                                                              op0=ALU.mult, op1=ALU.add),
                      reads=[Sf, EL[pb_], ps_ys], writes=[Sf])
            if k % 8 == 7:
                cb0 = (c // 8) * 8
                fw.dma(bass.AP(ysc[d], (s * SEQ + cb0 * CH) * D + hp * 128, [[D, CH], [CH * D, 8], [1, 128]]),
                       yst.t[:], yst, False)
            yield

        nsc = len(scans)

        def run_streams(a_idx, b_idx):
            a_chunks = [list(range(ln, NCH, NLANE)) for ln in range(NLANE)] if a_idx is not None else [[] for _ in range(NLANE)]
            b_chunks = list(range(NCH)) if b_idx is not None else []
            gens = [None] * (NLANE + 1)
            while True:
                alive = False
                for gi in range(NLANE + 1):
                    for _rep in range(2 if gi == NLANE else 1):
                        if gens[gi] is None:
                            if gi < NLANE and a_chunks[gi]:
                                gens[gi] = gen_A(a_idx, a_chunks[gi].pop(0), gi)
                            elif gi == NLANE and b_chunks:
                                gens[gi] = gen_B(b_idx, b_chunks.pop(0))
                        if gens[gi] is not None:
                            alive = True
                            try:
                                next(gens[gi])
                            except StopIteration:
                                gens[gi] = None
                if not alive:
                    break

        load_scan(0)
        run_streams(0, None)
        for idx in range(nsc):
            if idx + 1 < nsc:
                load_scan(idx + 1)
            run_streams(idx + 1 if idx + 1 < nsc else None, idx)
        fw.end_phase()

        fw.begin_phase("P6")
        d0 = fw.sb("d0", [128, 3968])
        fw.op("pool", lambda e: e.iota(d0.t[:], pattern=[[1, 3968]], base=-1920, channel_multiplier=-1,
                                       allow_small_or_imprecise_dtypes=True), writes=[d0])
        fw.op("dve", lambda e: e.scalar_tensor_tensor(out=d0.t[:], in0=d0.t[:], scalar=-1.0, in1=d0.t[:],
                                                      op0=ALU.mult, op1=ALU.max), reads=[d0], writes=[d0])
        qT = fw.sb("qT", [128, 4, SEQ], BF16)
        kT = fw.sb("kT", [128, 4, SEQ], BF16)
        vball = fw.sb("vball", [128, 16, D], BF16)
        Eh = [fw.sb(f"Eh{i}", [128, 3968]) for i in range(2)]
        lnb = fw.sb("lnb", [128, 1])
        fw.op("pool", lambda e: e.memset(lnb.t[:], math.log(0.125)), writes=[lnb])
        qin = [fw.sb(f"qin{i}", [128, 512]) for i in range(2)]
        kin = [fw.sb(f"kin{i}", [128, 512]) for i in range(2)]
        vin = [fw.sb(f"vin{i}", [128, D]) for i in range(2)]
        rt1 = fw.sb("rt1", [128, 8, 32])
        rt2 = fw.sb("rt2", [128, 8, 32])
        qrb = fw.sb("qrb", [128, 512], BF16)
        krb = fw.sb("krb", [128, 512], BF16)
        ptq = fw.ps("ptq", [128, 8, 128], BF16)
        ptk = fw.ps("ptk", [128, 8, 128], BF16)
        psc = [fw.ps(f"psc{i}", [128, 512]) for i in range(3)]
        pyr = [fw.ps(f"pyr{i}", [128, 512]) for i in range(2)]
        At = [fw.sb(f"At{i}", [128, 512], BF16) for i in range(3)]
        yst = [fw.sb(f"yst{i}", [128, 512]) for i in range(2)]
        assert fw.sb_bytes < 190 * 1024, fw.sb_bytes
        ecnt = 0
        for s in range(NSEQ):
            for j in range(16):
                b = j % 2
                rows = slice(s * SEQ + j * 128, s * SEQ + (j + 1) * 128)
                fw.dma(qin[b].t[:], proj.ap()[rows, C_QB:C_QB + 512], qin[b], True)
                fw.dma(kin[b].t[:], proj.ap()[rows, C_KB:C_KB + 512], kin[b], True)
                fw.dma(vin[b].t[:], proj.ap()[rows, C_VB:C_VB + D], vin[b], True)
                cosb = bc(cos_t.t[:, j, :].unsqueeze(1), [128, 8, 32])
                sinb = bc(sin_t.t[:, j, :].unsqueeze(1), [128, 8, 32])
                for (src, dst) in ((qin[b], qrb), (kin[b], krb)):
                    s3 = src.t[:].rearrange("p (h k) -> p h k", h=8)
                    d3 = dst.t[:].rearrange("p (h k) -> p h k", h=8)
                    x1, x2 = s3[:, :, 0:32], s3[:, :, 32:64]
                    fw.op("dve", lambda e: e.tensor_tensor(out=rt1.t[:], in0=x1, in1=cosb, op=ALU.mult),
                          reads=[src, cos_t], writes=[rt1])
                    fw.op("dve", lambda e: e.tensor_tensor(out=rt2.t[:], in0=x2, in1=sinb, op=ALU.mult),
                          reads=[src, sin_t], writes=[rt2])
                    fw.op("dve", lambda e: e.tensor_tensor(out=d3[:, :, 0:32], in0=rt1.t[:], in1=rt2.t[:], op=ALU.subtract),
                          reads=[rt1, rt2], writes=[dst])
                    fw.op("dve", lambda e: e.tensor_tensor(out=rt1.t[:], in0=x1, in1=sinb, op=ALU.mult),
                          reads=[src, sin_t], writes=[rt1])
                    fw.op("dve", lambda e: e.tensor_tensor(out=rt2.t[:], in0=x2, in1=cosb, op=ALU.mult),
                          reads=[src, cos_t], writes=[rt2])
                    fw.op("dve", lambda e: e.tensor_tensor(out=d3[:, :, 32:64], in0=rt1.t[:], in1=rt2.t[:], op=ALU.add),
                          reads=[rt1, rt2], writes=[dst])
                fw.op("act", lambda e: e.copy(out=vball.t[:, j, :], in_=vin[b].t[:]), reads=[vin[b]], writes=[vball])
                for (src, pt, dstT) in ((qrb, ptq, qT), (krb, ptk, kT)):
                    for hp in range(4):
                        fw.op("pe", lambda e: e.transpose(out=pt.t[:, hp, :], in_=src.t[:, hp * 128:(hp + 1) * 128],
                                                          identity=ident.t[:]), reads=[src, ident], writes=[pt])
                    fw.op("act", lambda e: e.copy(out=dstT.t[:, :, j * 128:(j + 1) * 128], in_=pt.t[:, 0:4, :]),
                          reads=[pt], writes=[dstT])
            for h in range(8):
                hp, pb = h // 2, (h % 2) * 64
                lg = math.log1p(-(2.0 ** (-5 - h)))
                E = Eh[ecnt % 2]
                ecnt += 1
                fw.op("act", lambda e: e.activation(out=E.t[:], in_=d0.t[:], func=AF.Exp, scale=lg, bias=lnb.t[:, 0:1]),
                      reads=[d0, lnb], writes=[E])
                for nb in range(4):
                    n0 = nb * 512
                    py = pyr[nb % 2]
                    for mt in range(16):
                        m0 = mt * 128
                        ps_ = psc[mt % 3]
                        at = At[mt % 3]
                        fw.op("pe", lambda e: e.matmul(ps_.t[:], lhsT=kT.t[pb:pb + 64, hp, m0:m0 + 128],
                                                       rhs=qT.t[pb:pb + 64, hp, n0:n0 + 512], start=True, stop=True),
                              reads=[kT, qT], writes=[ps_])
                        c0 = n0 - m0 + 1920
                        fw.op("dve", lambda e: e.tensor_tensor(out=at.t[:], in0=ps_.t[:], in1=E.t[:, c0:c0 + 512],
                                                               op=ALU.mult), reads=[ps_, E], writes=[at])
                        for nt in range(4):
                            fw.op("pe", lambda e: e.matmul(py.t[:, nt * 128:(nt + 1) * 128],
                                                           lhsT=at.t[:, nt * 128:(nt + 1) * 128],
                                                           rhs=vball.t[:, mt, h * 128:(h + 1) * 128],
                                                           start=(mt == 0 and nt == 0), stop=(mt == 15),
                                                           skip_group_check=True),
                                  reads=[at, vball], writes=[py])
                    ys = yst[nb % 2]
                    fw.op("act", lambda e: e.copy(out=ys.t[:], in_=py.t[:]), reads=[py], writes=[ys])
                    fw.dma(bass.AP(yret, (s * SEQ + n0) * D + h * 128, [[D, 128], [128 * D, 4], [1, 128]]),
                           ys.t[:].rearrange("p (n v) -> p n v", n=4), ys, False)
        fw.end_phase()

        fw.begin_phase("P7")
        lgb = fw.sb("lgb", [128, D])
        lbb = fw.sb("lbb", [128, D])
        rgb = fw.sb("rgb", [128, D])
        fw.dma(lgb.t[:], row_bc(lnx_gain, l * D, D), lgb, True)
        fw.dma(lbb.t[:], row_bc(lnx_bias, l * D, D), lbb, True)
        fw.dma(rgb.t[:], row_bc(ret_norm_gain, l * D, D), rgb, True)
        if last:
            fgb = fw.sb("fgb", [128, D])
            fw.dma(fgb.t[:], row_bc(final_gain, 0, D), fgb, True)
        wsg = [fw.sb(f"wsg{i}", [128, 2, D]) for i in range(2)]
        Wb = {}
        wi = 0
        for nm, wt in (("a", w_branch_a), ("b", w_branch_b), ("o", w_out)):
            Wb[nm] = fw.sb("W" + nm, [128, 8, D], BF16)
            for c in range(4):
                sg = wsg[wi % 2]
                wi += 1
                fw.dma(sg.t[:], bass.AP(wt, l * D * D + c * 256 * D, [[D, 128], [128 * D, 2], [1, D]]), sg, True)
                fw.op("pool", lambda e: e.tensor_copy(out=Wb[nm].t[:, 2 * c:2 * c + 2, :], in_=sg.t[:]),
                      reads=[sg], writes=[Wb[nm]])
        names = ("yf", "yb", "bon", "ga", "yr", "gb", "ma", "mb", "xt")
        L = {nm: fw.sb("L" + nm, [128, D]) for nm in names}
        y = fw.sb("y", [128, D])
        sq = fw.sb("sq", [128, D])
        st16 = fw.sb("st16", [128, 16])
        st8 = fw.sb("st8", [128, 8])
        yabf = fw.sb("yabf", [128, D], BF16)
        ybbf = fw.sb("ybbf", [128, D], BF16)
        mgbf = fw.sb("mgbf", [128, D], BF16)
        yaT = fw.sb("yaT", [128, 8, 128], BF16)
        ybT = fw.sb("ybT", [128, 8, 128], BF16)
        mgT = fw.sb("mgT", [128, 8, 128], BF16)
        ptp = fw.ps("ptp", [128, 8, 128], BF16)
        pa = fw.ps("pa", [128, D])
        pbb = fw.ps("pbb", [128, D])
        po = fw.ps("po", [128, D])
        xo = fw.sb("xo", [128, D])
        fss = fw.sb("fss", [128, 1])
        assert fw.sb_bytes < 190 * 1024, fw.sb_bytes

        def head_norm(src, nh, hd, eps, stt):
            s3 = src.t[:].rearrange("p (h k) -> p h k", h=nh)
            q3 = sq.t[:].rearrange("p (h k) -> p h k", h=nh)
            fw.op("dve", lambda e: e.tensor_reduce(out=stt.t[:], in_=s3, axis=AX.X, op=ALU.add), reads=[src], writes=[stt])
            fw.op("dve", lambda e: e.tensor_scalar(out=stt.t[:], in0=stt.t[:], scalar1=-1.0 / hd, scalar2=None,
                                                   op0=ALU.mult), reads=[stt], writes=[stt])
            fw.op("dve", lambda e: e.tensor_tensor(out=s3, in0=s3, in1=bc(stt.t[:].unsqueeze(2), [128, nh, hd]),
                                                   op=ALU.add), reads=[src, stt], writes=[src])
            fw.op("pool", lambda e: e.tensor_tensor(out=sq.t[:], in0=src.t[:], in1=src.t[:], op=ALU.mult),
                  reads=[src], writes=[sq])
            fw.op("dve", lambda e: e.tensor_reduce(out=stt.t[:], in_=q3, axis=AX.X, op=ALU.add), reads=[sq], writes=[stt])
            rstd_from(stt, hd, eps)
            fw.op("dve", lambda e: e.tensor_tensor(out=s3, in0=s3, in1=bc(stt.t[:].unsqueeze(2), [128, nh, hd]),
                                                   op=ALU.mult), reads=[src, stt], writes=[src])

        for ti in range(NT):
            rows = slice(ti * 128, (ti + 1) * 128)
            fw.dma(L["yf"].t[:], ysc[0].ap()[rows, :], L["yf"], True)
            fw.dma(L["yb"].t[:], ysc[1].ap()[rows, :], L["yb"], True)
            fw.dma(L["bon"].t[:], bonus.ap()[rows, :], L["bon"], True)
            fw.dma(L["ga"].t[:], proj.ap()[rows, C_GA:C_GA + D], L["ga"], True)
            fw.dma(L["yr"].t[:], yret.ap()[rows, :], L["yr"], True)
            fw.dma(L["gb"].t[:], proj.ap()[rows, C_GB:C_GB + D], L["gb"], True)
            fw.dma(L["ma"].t[:], proj.ap()[rows, C_MA:C_MA + D], L["ma"], True)
            fw.dma(L["mb"].t[:], proj.ap()[rows, C_MB:C_MB + D], L["mb"], True)
            fw.dma(L["xt"].t[:], xcur.ap()[rows, :], L["xt"], True)
            fw.op("dve", lambda e: e.tensor_tensor(out=y.t[:], in0=L["yf"].t[:], in1=L["yb"].t[:], op=ALU.add),
                  reads=[L["yf"], L["yb"]], writes=[y])
            head_norm(y, 16, 64, GN_EPS, st16)
            fw.op("dve", lambda e: e.tensor_tensor(out=y.t[:], in0=y.t[:], in1=lgb.t[:], op=ALU.mult),
                  reads=[y, lgb], writes=[y])
            fw.op("pool", lambda e: e.tensor_tensor(out=L["bon"].t[:], in0=L["bon"].t[:], in1=lbb.t[:], op=ALU.add),
                  reads=[L["bon"], lbb], writes=[L["bon"]])
            fw.op("dve", lambda e: e.tensor_tensor(out=y.t[:], in0=y.t[:], in1=L["bon"].t[:], op=ALU.add),
                  reads=[y, L["bon"]], writes=[y])
            fw.op("act", lambda e: e.activation(out=L["ga"].t[:], in_=L["ga"].t[:], func=AF.Silu),
                  reads=[L["ga"]], writes=[L["ga"]])
            fw.op("dve", lambda e: e.tensor_tensor(out=yabf.t[:], in0=y.t[:], in1=L["ga"].t[:], op=ALU.mult),
                  reads=[y, L["ga"]], writes=[yabf])
            head_norm(L["yr"], 8, 128, NORM_EPS, st8)
            fw.op("act", lambda e: e.activation(out=L["gb"].t[:], in_=L["gb"].t[:], func=AF.Silu),
                  reads=[L["gb"]], writes=[L["gb"]])
            fw.op("pool", lambda e: e.tensor_tensor(out=L["gb"].t[:], in0=L["gb"].t[:], in1=rgb.t[:], op=ALU.mult),
                  reads=[L["gb"], rgb], writes=[L["gb"]])
            fw.op("dve", lambda e: e.tensor_tensor(out=ybbf.t[:], in0=L["yr"].t[:], in1=L["gb"].t[:], op=ALU.mult),
                  reads=[L["yr"], L["gb"]], writes=[ybbf])
            for (src, dT, pacc, W) in ((yabf, yaT, pa, Wb["a"]), (ybbf, ybT, pbb, Wb["b"])):
                for kc in range(8):
                    fw.op("pe", lambda e: e.transpose(out=ptp.t[:, kc, :], in_=src.t[:, kc * 128:(kc + 1) * 128],
                                                      identity=ident.t[:]), reads=[src, ident], writes=[ptp])
                fw.op("act", lambda e: e.copy(out=dT.t[:], in_=ptp.t[:]), reads=[ptp], writes=[dT])
                for hb in range(2):
                    for kc in range(8):
                        fw.op("pe", lambda e: e.matmul(pacc.t[:, hb * 512:(hb + 1) * 512], lhsT=dT.t[:, kc, :],
                                                       rhs=W.t[:, kc, hb * 512:(hb + 1) * 512],
                                                       start=(kc == 0), stop=(kc == 7)), reads=[dT, W], writes=[pacc])
            fw.op("act", lambda e: e.activation(out=L["ma"].t[:], in_=L["ma"].t[:], func=AF.Sigmoid),
                  reads=[L["ma"]], writes=[L["ma"]])
            fw.op("act", lambda e: e.activation(out=L["mb"].t[:], in_=L["mb"].t[:], func=AF.Sigmoid),
                  reads=[L["mb"]], writes=[L["mb"]])
            fw.op("dve", lambda e: e.tensor_tensor(out=L["ma"].t[:], in0=pa.t[:], in1=L["ma"].t[:], op=ALU.mult),
                  reads=[pa, L["ma"]], writes=[L["ma"]])
            fw.op("dve", lambda e: e.tensor_tensor(out=L["mb"].t[:], in0=pbb.t[:], in1=L["mb"].t[:], op=ALU.mult),
                  reads=[pbb, L["mb"]], writes=[L["mb"]])
            fw.op("pool", lambda e: e.tensor_tensor(out=mgbf.t[:], in0=L["ma"].t[:], in1=L["mb"].t[:], op=ALU.add),
                  reads=[L["ma"], L["mb"]], writes=[mgbf])
            for kc in range(8):
                fw.op("pe", lambda e: e.transpose(out=ptp.t[:, kc, :], in_=mgbf.t[:, kc * 128:(kc + 1) * 128],
                                                  identity=ident.t[:]), reads=[mgbf, ident], writes=[ptp])
            fw.op("act", lambda e: e.copy(out=mgT.t[:], in_=ptp.t[:]), reads=[ptp], writes=[mgT])
            for hb in range(2):
                for kc in range(8):
                    fw.op("pe", lambda e: e.matmul(po.t[:, hb * 512:(hb + 1) * 512], lhsT=mgT.t[:, kc, :],
                                                   rhs=Wb["o"].t[:, kc, hb * 512:(hb + 1) * 512],
                                                   start=(kc == 0), stop=(kc == 7)), reads=[mgT, Wb["o"]], writes=[po])
            fw.op("dve", lambda e: e.tensor_tensor(out=xo.t[:], in0=po.t[:], in1=L["xt"].t[:], op=ALU.add),
                  reads=[po, L["xt"]], writes=[xo])
            if not last:
                fw.dma(xnext.ap()[rows, :], xo.t[:], xo, False)
            else:
                fw.op("pool", lambda e: e.memset(fss.t[:], 0.0), writes=[fss])
                fw.op("act", lambda e: e.activation(out=sq.t[:], in_=xo.t[:], func=AF.Square, accum_out=fss.t[:]),
                      reads=[xo, fss], writes=[sq, fss])
                rstd_from(fss, D, NORM_EPS)
                fw.op("dve", lambda e: e.scalar_tensor_tensor(out=xo.t[:], in0=xo.t[:], scalar=fss.t[:, 0:1],
                                                              in1=fgb.t[:], op0=ALU.mult, op1=ALU.mult),
                      reads=[xo, fss, fgb], writes=[xo])
                fw.dma(out.ap()[rows, :], xo.t[:], xo, False)
            if dbg and n_layers < DEPTH and l == n_layers - 1:
                fw.dma(out.ap()[rows, :], xo.t[:], xo, False)
        fw.end_phase()

    print(f"[build] ins={fw.n_ins} waits={fw.n_wait} sems={fw.nsem}", flush=True)
    return nc


_INPUT_NAMES = ["norm_gain", "w_in", "w_vres_down", "shift_prev", "shift_next", "w_decay_up", "decay_bias",
                "w_iclr_up", "iclr_bias", "w_vres_up", "vres_bias", "k_k", "k_a", "r_k", "lnx_gain", "lnx_bias",
                "w_branch_a", "ret_norm_gain", "w_branch_b", "w_out", "final_gain"]


def make_in_maps(inputs):
    x = np.ascontiguousarray(np.asarray(inputs["x"], dtype=np.float32))
    shared = {}
    for nm in _INPUT_NAMES:
        a = np.ascontiguousarray(np.asarray(inputs[nm], dtype=np.float32))
        if nm == "r_k":
            a = a.reshape(DEPTH, D)
        if nm == "final_gain":
            a = a.reshape(1, D)
        shared[nm] = a
    in_maps = []
    for c in range(NCORE):
        m = dict(shared)
        m["x"] = x[c * NSEQ:(c + 1) * NSEQ].reshape(NTOK, D)
        in_maps.append(m)
    return in_maps


def kernel(**inputs):
    nc = build()
    in_maps = make_in_maps(inputs)
    res = run_bass_kernel_spmd(nc, in_maps, core_ids=list(range(NCORE)))
    outs = [np.asarray(r["out"], dtype=np.float32).reshape(NSEQ, SEQ, D) for r in res.results]
    return np.concatenate(outs, axis=0)
```

```python
import math
import os
from contextlib import ExitStack

import numpy as np
import concourse.bass as bass
import concourse.mybir as mybir
from concourse.bass_utils import run_bass_kernel_spmd

F32 = mybir.dt.float32
BF16 = mybir.dt.bfloat16
ALU = mybir.AluOpType
AF = mybir.ActivationFunctionType
AX = mybir.AxisListType

NCORE = 8
DEPTH = 4
D = 1024
SEQ = 2048
NSEQ = 2
NTOK = NSEQ * SEQ
NT = NTOK // 128
NIN = 9472
NINX = NIN + 32
SHW = 3328
C_R, C_K, C_V, C_CODE = 0, 1024, 2048, 3072
C_GA, C_QB, C_KB, C_VB, C_GB, C_MA, C_MB, C_VL = 3328, 4352, 4864, 5376, 6400, 7424, 8448, 9472
NORM_EPS = 1e-6
GN_EPS = 64e-5
TB = 32


class Buf:
    def __init__(self, t, name=""):
        self.t = t
        self.name = name
        self.w = {}
        self.r = {}
        self.dsem = None


class FW:
    SEM_MAX = 30000

    def __init__(self, nc):
        self.nc = nc
        self.eng = {"pe": nc.tensor, "act": nc.scalar, "dve": nc.vector, "pool": nc.gpsimd, "sp": nc.sync}
        self.sems = {}
        self.cnt = {}
        self.esem = {}
        self.seen = {e: {} for e in self.eng}
        self.nsem = 0
        self.n_wait = 0
        self.n_ins = 0
        self.free_dsems = []
        self.phase_bufs = []
        self.stack = None
        self.sb_bytes = 0
        self.mute = False
        self.skip = ()
        for e in ("pe", "act", "dve", "pool"):
            self.esem[e] = self.new_sem("e" + e)

    def new_sem(self, name):
        key = f"{name}_{self.nsem}"
        self.nsem += 1
        self.sems[key] = self.nc.alloc_semaphore(name=key)
        self.cnt[key] = 0
        return key

    def _get_dsem(self):
        while self.free_dsems:
            k = self.free_dsems.pop()
            if self.cnt[k] + 16 * 64 < self.SEM_MAX:
                return k
        return self.new_sem("d")

    def begin_phase(self, name=""):
        self.mute = name in self.skip
        self.stack = ExitStack()
        self.phase_bufs = []
        self.sb_bytes = 0

    def end_phase(self):
        self.barrier()
        for b in self.phase_bufs:
            if b.dsem is not None:
                self.free_dsems.append(b.dsem)
        self.stack.close()
        self.stack = None

    def sb(self, name, shape, dt=F32):
        self.uid = getattr(self, "uid", 0) + 1
        t = self.stack.enter_context(self.nc.sbuf_tensor(f"{name}_{self.uid}", list(shape), dt))
        n = 1
        for s in shape[1:]:
            n *= s
        self.sb_bytes += n * (2 if dt == BF16 else 4)
        b = Buf(t, name)
        self.phase_bufs.append(b)
        return b

    def ps(self, name, shape, dt=F32):
        self.uid = getattr(self, "uid", 0) + 1
        t = self.stack.enter_context(self.nc.psum_tensor(f"{name}_{self.uid}", list(shape), dt))
        b = Buf(t, name)
        b.psum = True
        self.phase_bufs.append(b)
        return b

    def _wait(self, e, deps, raw=()):
        for key, val in deps.items():
            if val <= 0 or self.seen[e].get(key, 0) >= val:
                continue
            if key.startswith("e" + e + "_") and key not in raw:
                continue
            self.eng[e].wait_ge(self.sems[key], val)
            self.seen[e][key] = val
            self.n_wait += 1

    @staticmethod
    def _merge(d, key, val):
        if d.get(key, 0) < val:
            d[key] = val

    def _deps(self, reads, writes, skip=None):
        deps = {}
        raw = set()
        for b in reads:
            for k, v in b.w.items():
                if k != skip:
                    self._merge(deps, k, v)
                    raw.add(k)
            if getattr(b, "psum", False):
                for k, v in b.r.items():
                    self._merge(deps, k, v)
        for b in writes:
            for k, v in b.w.items():
                if k != skip:
                    self._merge(deps, k, v)
            for k, v in b.r.items():
                self._merge(deps, k, v)
        self._raw = raw
        return deps

    def op(self, e, fn, reads=(), writes=()):
        if self.mute:
            return None
        deps = self._deps(reads, writes)
        self._wait(e, deps, self._raw if e != "pe" else ())
        if self.cnt[self.esem[e]] >= self.SEM_MAX:
            self.esem[e] = self.new_sem("e" + e)
        key = self.esem[e]
        ins = fn(self.eng[e])
        self.cnt[key] += 1
        val = self.cnt[key]
        ins.then_inc(self.sems[key], 1)
        self.n_ins += 1
        for b in reads:
            self._merge(b.r, key, val)
        for b in writes:
            b.w = {key: val}
            b.r = {}
        return ins

    def dma(self, out_ap, in_ap, sbuf, load, q="sp"):
        if self.mute:
            return None
        if sbuf.dsem is None or self.cnt[sbuf.dsem] + 16 >= self.SEM_MAX:
            sbuf.dsem = self._get_dsem()
        sem = sbuf.dsem
        if load:
            deps = self._deps([], [sbuf], skip=sem)
        else:
            deps = self._deps([sbuf], [])
        self._wait(q, deps)
        ins = self.eng[q].dma_start(out=out_ap, in_=in_ap)
        self.cnt[sem] += 16
        val = self.cnt[sem]
        ins.then_inc(self.sems[sem], 16)
        self.n_ins += 1
        if load:
            sbuf.w[sem] = val
            sbuf.r = {}
        else:
            self._merge(sbuf.r, sem, val)
        return ins

    def barrier(self):
        for e in self.eng:
            for key, val in self.cnt.items():
                if val > 0 and self.seen[e].get(key, 0) < val and not key.startswith("e" + e + "_"):
                    self.eng[e].wait_ge(self.sems[key], val)
                    self.seen[e][key] = val
                    self.n_wait += 1


def bc(ap, shape):
    return ap.to_broadcast(list(shape))


def row_bc(t, off, n):
    return bass.AP(t, off, [[0, 128], [1, n]])


def build(n_layers=DEPTH, dbg=False, skip=()):
    nc = bass.Bass("TRN2", target_bir_lowering=False)
    fw = FW(nc)
    fw.skip = tuple(skip)
    okind = "ExternalOutput" if dbg else "Internal"

    def din(name, shape):
        return nc.dram_tensor(name, list(shape), F32, kind="ExternalInput")

    x_in = din("x", [NTOK, D])
    norm_gain = din("norm_gain", [DEPTH, D])
    w_in = din("w_in", [DEPTH, D, NIN])
    w_vres_down = din("w_vres_down", [DEPTH - 1, D, 32])
    shift_prev = din("shift_prev", [DEPTH, SHW])
    shift_next = din("shift_next", [DEPTH, SHW])
    w_decay_up = din("w_decay_up", [DEPTH, 2, 64, D])
    decay_bias = din("decay_bias", [DEPTH, 2, D])
    w_iclr_up = din("w_iclr_up", [DEPTH, 2, 64, D])
    iclr_bias = din("iclr_bias", [DEPTH, 2, D])
    w_vres_up = din("w_vres_up", [DEPTH - 1, 32, D])
    vres_bias = din("vres_bias", [DEPTH - 1, D])
    k_k = din("k_k", [DEPTH, D])
    k_a = din("k_a", [DEPTH, D])
    r_k = din("r_k", [DEPTH, D])
    lnx_gain = din("lnx_gain", [DEPTH, D])
    lnx_bias = din("lnx_bias", [DEPTH, D])
    w_branch_a = din("w_branch_a", [DEPTH, D, D])
    ret_norm_gain = din("ret_norm_gain", [DEPTH, D])
    w_branch_b = din("w_branch_b", [DEPTH, D, D])
    w_out = din("w_out", [DEPTH, D, D])
    final_gain = din("final_gain", [1, D])
    out = nc.dram_tensor("out", [NTOK, D], F32, kind="ExternalOutput")

    def scr(name, shape):
        return nc.dram_tensor(name, list(shape), F32, kind=okind)

    proj = scr("proj", [NTOK, NINX])
    xs = [scr("xs0", [NTOK, D]), scr("xs1", [NTOK, D])]
    s_r = scr("s_r", [NTOK, D])
    s_a = scr("s_a", [NTOK, D])
    s_v = scr("s_v", [NTOK, D])
    s_w = [scr("s_w0", [NTOK, D]), scr("s_w1", [NTOK, D])]
    s_b = [scr("s_b0", [NTOK, D]), scr("s_b1", [NTOK, D])]
    s_k = [scr("s_k0", [NTOK, D]), scr("s_k1", [NTOK, D])]
    vfirst = scr("vfirst", [NTOK, D])
    bonus = scr("bonus", [NTOK, D])
    ysc = [scr("ysc0", [NTOK, D]), scr("ysc1", [NTOK, D])]
    yret = scr("yret", [NTOK, D])

    def scrb(name, shape, dt=BF16):
        return nc.dram_tensor(name, list(shape), dt, kind="Internal")

    fm_a = [scrb(f"fm_a{d}", [8, 128, NTOK]) for d in range(2)]
    fm_r = [scrb(f"fm_r{d}", [8, 128, NTOK]) for d in range(2)]
    fm_b = [scrb(f"fm_b{d}", [8, 128, NTOK]) for d in range(2)]
    fm_k = [scrb(f"fm_k{d}", [8, 128, NTOK]) for d in range(2)]
    tm_b = [scrb(f"tm_b{d}", [NTOK, D]) for d in range(2)]
    tm_k = [scrb(f"tm_k{d}", [NTOK, D]) for d in range(2)]
    tm_v = scrb("tm_v", [NTOK, D])
    elc = [scrb(f"elc{d}", [128, 512], F32) for d in range(2)]

    ident_t = nc.alloc_sbuf_tensor("ident", [128, 128], BF16)
    ident = Buf(ident_t, "ident")
    cos_t = Buf(nc.alloc_sbuf_tensor("cos_t", [128, 16, 32], F32), "cos")
    sin_t = Buf(nc.alloc_sbuf_tensor("sin_t", [128, 16, 32], F32), "sin")

    fw.begin_phase()
    io = fw.sb("io", [128, 128])
    fw.op("pool", lambda e: e.iota(io.t[:], pattern=[[1, 128]], base=0, channel_multiplier=-1,
                                   allow_small_or_imprecise_dtypes=True), writes=[io])
    fw.op("dve", lambda e: e.tensor_scalar(out=ident.t[:], in0=io.t[:], scalar1=0.0, scalar2=None,
                                           op0=ALU.is_equal), reads=[io], writes=[ident])
    pos = fw.sb("pos", [128, 16])
    fw.op("pool", lambda e: e.iota(pos.t[:], pattern=[[128, 16]], base=0, channel_multiplier=1,
                                   allow_small_or_imprecise_dtypes=True), writes=[pos])
    freq = fw.sb("freq", [128, 32])
    fr = np.power(np.float32(10000.0), -np.arange(32, dtype=np.float32) / np.float32(32)).astype(np.float32)
    for i in range(32):
        fw.op("pool", lambda e: e.memset(freq.t[:, i:i + 1], float(fr[i])), writes=[freq])
    ang = fw.sb("ang", [128, 16, 32])
    fw.op("dve", lambda e: e.tensor_tensor(out=ang.t[:], in0=bc(pos.t[:].unsqueeze(2), [128, 16, 32]),
                                           in1=bc(freq.t[:].unsqueeze(1), [128, 16, 32]), op=ALU.mult),
          reads=[pos, freq], writes=[ang])
    tmpa = fw.sb("tmpa", [128, 16, 32])
    negpi = fw.sb("negpi", [128, 1])
    fw.op("pool", lambda e: e.memset(negpi.t[:], -math.pi), writes=[negpi])
    tmpi = fw.sb("tmpi", [128, 16, 32], mybir.dt.int32)
    tmpf = fw.sb("tmpf", [128, 16, 32])
    for (shift, dst) in ((math.pi, sin_t), (1.5 * math.pi, cos_t)):
        fw.op("dve", lambda e: e.tensor_scalar(out=tmpa.t[:], in0=ang.t[:], scalar1=1.0 / (2 * math.pi),
                                               scalar2=shift / (2 * math.pi), op0=ALU.mult, op1=ALU.add),
              reads=[ang], writes=[tmpa])
        fw.op("dve", lambda e: e.tensor_copy(out=tmpi.t[:], in_=tmpa.t[:]), reads=[tmpa], writes=[tmpi])
        fw.op("dve", lambda e: e.tensor_copy(out=tmpf.t[:], in_=tmpi.t[:]), reads=[tmpi], writes=[tmpf])
        fw.op("dve", lambda e: e.tensor_tensor(out=tmpa.t[:], in0=tmpa.t[:], in1=tmpf.t[:], op=ALU.subtract),
              reads=[tmpa, tmpf], writes=[tmpa])
        fw.op("dve", lambda e: e.tensor_scalar(out=tmpf.t[:], in0=tmpa.t[:], scalar1=0.0, scalar2=None, op0=ALU.is_lt),
              reads=[tmpa], writes=[tmpf])
        fw.op("dve", lambda e: e.tensor_tensor(out=tmpa.t[:], in0=tmpa.t[:], in1=tmpf.t[:], op=ALU.add),
              reads=[tmpa, tmpf], writes=[tmpa])
        fw.op("act", lambda e: e.activation(out=dst.t[:], in_=tmpa.t[:], func=AF.Sin, scale=2 * math.pi,
                                            bias=negpi.t[:, 0:1]), reads=[tmpa, negpi], writes=[dst])
    fw.end_phase()

    evac_rr = [0]

    def evac(out_ap, in_ap, reads, writes):
        evac_rr[0] ^= 1
        if evac_rr[0]:
            fw.op("act", lambda e: e.copy(out=out_ap, in_=in_ap), reads=reads, writes=writes)
        else:
            fw.op("dve", lambda e: e.tensor_copy(out=out_ap, in_=in_ap), reads=reads, writes=writes)

    def rstd_from(ss, n, eps):
        fw.op("dve", lambda e: e.tensor_scalar(out=ss.t[:], in0=ss.t[:], scalar1=1.0 / n, scalar2=eps,
                                               op0=ALU.mult, op1=ALU.add), reads=[ss], writes=[ss])
        fw.op("act", lambda e: e.activation(out=ss.t[:], in_=ss.t[:], func=AF.Sqrt), reads=[ss], writes=[ss])
        fw.op("dve", lambda e: e.reciprocal(out=ss.t[:], in_=ss.t[:]), reads=[ss], writes=[ss])

    def transpose8(src_bf, dstT, ptr, n=8):
        for kc in range(n):
            fw.op("pe", lambda e: e.transpose(out=ptr.t[:, kc, :], in_=src_bf.t[:, kc * 128:(kc + 1) * 128],
                                              identity=ident.t[:]), reads=[src_bf, ident], writes=[ptr])
        evac(dstT, ptr.t[:, 0:n, :], [ptr], [])

    for l in range(n_layers):
        xcur = x_in if l == 0 else xs[(l - 1) % 2]
        xnext = xs[l % 2]
        ncol = NIN if l == 0 else NINX
        last = (l == DEPTH - 1)

        fw.begin_phase("P12")
        hnT = fw.sb("hnT", [128, 8, NTOK], BF16)
        gain = fw.sb("gain", [128, D])
        fw.dma(gain.t[:], row_bc(norm_gain, l * D, D), gain, True)
        xt = [fw.sb(f"xt{i}", [128, D]) for i in range(2)]
        junk = fw.sb("junk", [128, D])
        ssq = [fw.sb(f"ssq{i}", [128, 1]) for i in range(2)]
        hn = [fw.sb(f"hn{i}", [128, D], BF16) for i in range(2)]
        ptr = [fw.ps(f"ptr{i}", [128, 8, 128], BF16) for i in range(2)]
        for ti in range(NT):
            b = ti % 2
            fw.dma(xt[b].t[:], xcur.ap()[ti * 128:(ti + 1) * 128, :], xt[b], True)
            fw.op("pool", lambda e: e.memset(ssq[b].t[:], 0.0), writes=[ssq[b]])
            fw.op("act", lambda e: e.activation(out=junk.t[:], in_=xt[b].t[:], func=AF.Square,
                                                accum_out=ssq[b].t[:]), reads=[xt[b], ssq[b]], writes=[junk, ssq[b]])
            rstd_from(ssq[b], D, NORM_EPS)
            fw.op("dve", lambda e: e.scalar_tensor_tensor(out=hn[b].t[:], in0=xt[b].t[:], scalar=ssq[b].t[:, 0:1],
                                                          in1=gain.t[:], op0=ALU.mult, op1=ALU.mult),
                  reads=[xt[b], ssq[b], gain], writes=[hn[b]])
            for kc in range(8):
                fw.op("pe", lambda e: e.transpose(out=ptr[b].t[:, kc, :], in_=hn[b].t[:, kc * 128:(kc + 1) * 128],
                                                  identity=ident.t[:]), reads=[hn[b], ident], writes=[ptr[b]])
            evac(hnT.t[:, :, ti * 128:(ti + 1) * 128], ptr[b].t[:], [ptr[b]], [hnT])

        wst = [fw.sb(f"wst{i}", [128, 8, 512]) for i in range(2)]
        wbf = [fw.sb(f"wbf{i}", [128, 8, 512], BF16) for i in range(2)]
        pp = [fw.ps(f"pp{i}", [128, 512]) for i in range(4)]
        stg = [fw.sb(f"stg{i}", [128, 512]) for i in range(4)]
        ncb = (ncol + 511) // 512
        it = 0
        for cb in range(ncb):
            c0 = cb * 512
            cw = min(512, ncol - c0)
            wb = cb % 2
            cmain = min(cw, NIN - c0)
            fw.dma(wst[wb].t[:, :, 0:cmain],
                   bass.AP(w_in, l * D * NIN + c0, [[NIN, 128], [128 * NIN, 8], [1, cmain]]), wst[wb], True)
            if cw > cmain:
                fw.dma(wst[wb].t[:, :, cmain:cw],
                       bass.AP(w_vres_down, (l - 1) * D * 32, [[32, 128], [128 * 32, 8], [1, 32]]), wst[wb], True)
            fw.op("pool", lambda e: e.tensor_copy(out=wbf[wb].t[:, :, 0:cw], in_=wst[wb].t[:, :, 0:cw]),
                  reads=[wst[wb]], writes=[wbf[wb]])
            for ti in range(NT):
                pb = it % 4
                it += 1
                for kc in range(8):
                    fw.op("pe", lambda e: e.matmul(pp[pb].t[:, 0:cw], lhsT=hnT.t[:, kc, ti * 128:(ti + 1) * 128],
                                                   rhs=wbf[wb].t[:, kc, 0:cw], start=(kc == 0), stop=(kc == 7)),
                          reads=[hnT, wbf[wb]], writes=[pp[pb]])
                evac(stg[pb].t[:, 0:cw], pp[pb].t[:, 0:cw], [pp[pb]], [stg[pb]])
                fw.dma(proj.ap()[ti * 128:(ti + 1) * 128, c0:c0 + cw], stg[pb].t[:, 0:cw], stg[pb], False)
        fw.end_phase()

        fw.begin_phase("P3")
        mup = fw.sb("mup", [128, SHW])
        mun = fw.sb("mun", [128, SHW])
        fw.dma(mup.t[:], row_bc(shift_prev, l * SHW, SHW), mup, True)
        fw.dma(mun.t[:], row_bc(shift_next, l * SHW, SHW), mun, True)
        decb = [fw.sb(f"decb{d}", [128, D]) for d in range(2)]
        iclb = [fw.sb(f"iclb{d}", [128, D]) for d in range(2)]
        for d in range(2):
            fw.dma(decb[d].t[:], row_bc(decay_bias, (l * 2 + d) * D, D), decb[d], True)
            fw.dma(iclb[d].t[:], row_bc(iclr_bias, (l * 2 + d) * D, D), iclb[d], True)
        kkb = fw.sb("kkb", [128, D])
        kab = fw.sb("kab", [128, D])
        rkb = fw.sb("rkb", [128, D])
        fw.dma(kkb.t[:], row_bc(k_k, l * D, D), kkb, True)
        fw.dma(kab.t[:], row_bc(k_a, l * D, D), kab, True)
        fw.dma(rkb.t[:], row_bc(r_k, l * D, D), rkb, True)
        wstg = fw.sb("wstg", [128, D])
        wdec = fw.sb("wdec", [128, D], BF16)
        wicl = fw.sb("wicl", [128, D], BF16)
        fw.dma(wstg.t[:], bass.AP(w_decay_up, l * 128 * D, [[D, 128], [1, D]]), wstg, True)
        fw.op("dve", lambda e: e.tensor_copy(out=wdec.t[:], in_=wstg.t[:]), reads=[wstg], writes=[wdec])
        fw.dma(wstg.t[:], bass.AP(w_iclr_up, l * 128 * D, [[D, 128], [1, D]]), wstg, True)
        fw.op("dve", lambda e: e.tensor_copy(out=wicl.t[:], in_=wstg.t[:]), reads=[wstg], writes=[wicl])
        if l > 0:
            vrb = fw.sb("vrb", [128, D])
            fw.dma(vrb.t[:], row_bc(vres_bias, (l - 1) * D, D), vrb, True)
            wvr = fw.sb("wvr", [32, D], BF16)
            fw.dma(wstg.t[0:32, :], bass.AP(w_vres_up, (l - 1) * 32 * D, [[D, 32], [1, D]]), wstg, True)
            fw.op("dve", lambda e: e.tensor_copy(out=wvr.t[:], in_=wstg.t[0:32, :]), reads=[wstg], writes=[wvr])
            vl = fw.sb("vl", [128, 32])
            vf = fw.sb("vf", [128, D])
            vg = fw.sb("vg", [128, D])
            v_o = fw.sb("v_o", [128, D])
            pvr = fw.ps("pvr", [128, D])
        c0b = fw.sb("c0b", [128, SHW])
        cmb = fw.sb("cmb", [128, SHW])
        cpb = fw.sb("cpb", [128, SHW])
        cdbf = fw.sb("cdbf", [128, 384], BF16)
        cdT = fw.sb("cdT", [128, 384], BF16)
        pT = fw.ps("pT", [128, 1024], BF16)
        pdec = fw.ps("pdec", [128, D])
        picl = fw.ps("picl", [128, D])
        kk = fw.sb("kk", [128, D])
        tmp = fw.sb("tmp", [128, D])
        rk = fw.sb("rk", [128, D])
        a_o = fw.sb("a_o", [128, D])
        al = fw.sb("al", [128, D])
        xd = fw.sb("xd", [128, D])
        w_o = [fw.sb(f"w_o{d}", [128, D]) for d in range(2)]
        b_o = [fw.sb(f"b_o{d}", [128, D]) for d in range(2)]
        kd_o = [fw.sb(f"kd_o{d}", [128, D]) for d in range(2)]
        bon_o = fw.sb("bon_o", [128, D])
        ss16 = fw.sb("ss16", [128, 16])
        sd = [fw.sb(f"sd{d}", [128, 16]) for d in range(2)]
        for ti in range(NT):
            j = ti % 16
            t0 = ti * 128
            rows = slice(t0, t0 + 128)
            fw.dma(c0b.t[:], proj.ap()[rows, 0:SHW], c0b, True)
            if j == 0:
                fw.op("pool", lambda e: e.memset(cmb.t[:], 0.0), writes=[cmb])
                fw.dma(cmb.t[1:128, :], proj.ap()[t0:t0 + 127, 0:SHW], cmb, True)
            else:
                fw.dma(cmb.t[:], proj.ap()[t0 - 1:t0 + 127, 0:SHW], cmb, True)
            if j == 15:
                fw.op("pool", lambda e: e.memset(cpb.t[:], 0.0), writes=[cpb])
                fw.dma(cpb.t[0:127, :], proj.ap()[t0 + 1:t0 + 128, 0:SHW], cpb, True)
            else:
                fw.dma(cpb.t[:], proj.ap()[t0 + 1:t0 + 129, 0:SHW], cpb, True)
            if l > 0:
                fw.dma(vl.t[:], proj.ap()[rows, C_VL:C_VL + 32], vl, True)
                fw.dma(vf.t[:], vfirst.ap()[rows, :], vf, True)
            fw.op("dve", lambda e: e.tensor_tensor(out=cmb.t[:], in0=cmb.t[:], in1=c0b.t[:], op=ALU.subtract),
                  reads=[c0b, cmb], writes=[cmb])
            fw.op("dve", lambda e: e.tensor_tensor(out=cmb.t[:], in0=cmb.t[:], in1=mup.t[:], op=ALU.mult),
                  reads=[cmb, mup], writes=[cmb])
            fw.op("pool", lambda e: e.tensor_tensor(out=cpb.t[:], in0=cpb.t[:], in1=c0b.t[:], op=ALU.subtract),
                  reads=[c0b, cpb], writes=[cpb])
            fw.op("pool", lambda e: e.tensor_tensor(out=cpb.t[:], in0=cpb.t[:], in1=mun.t[:], op=ALU.mult),
                  reads=[cpb, mun], writes=[cpb])
            fw.op("dve", lambda e: e.tensor_tensor(out=c0b.t[:], in0=c0b.t[:], in1=cmb.t[:], op=ALU.add),
                  reads=[c0b, cmb], writes=[c0b])
            fw.op("dve", lambda e: e.tensor_tensor(out=c0b.t[:], in0=c0b.t[:], in1=cpb.t[:], op=ALU.add),
                  reads=[c0b, cpb], writes=[c0b])
            sh = c0b
            r_ap = sh.t[:, C_R:C_R + D]
            k_ap = sh.t[:, C_K:C_K + D]
            v_ap = sh.t[:, C_V:C_V + D]
            fw.op("act", lambda e: e.activation(out=cdbf.t[:, 0:128], in_=sh.t[:, C_CODE:C_CODE + 128], func=AF.Tanh),
                  reads=[sh], writes=[cdbf])
            fw.op("act", lambda e: e.copy(out=cdbf.t[:, 128:256], in_=sh.t[:, C_CODE + 128:C_CODE + 256]),
                  reads=[sh], writes=[cdbf])
            ncode = 2
            if l > 0:
                fw.op("act", lambda e: e.copy(out=cdbf.t[:, 256:288], in_=vl.t[:]), reads=[vl], writes=[cdbf])
            for c in range(2):
                fw.op("pe", lambda e: e.transpose(out=pT.t[:, c * 128:(c + 1) * 128], in_=cdbf.t[:, c * 128:(c + 1) * 128],
                                                  identity=ident.t[:]), reads=[cdbf, ident], writes=[pT])
            if l > 0:
                fw.op("pe", lambda e: e.transpose(out=pT.t[0:32, 256:384], in_=cdbf.t[:, 256:288],
                                                  identity=ident.t[:]), reads=[cdbf, ident], writes=[pT])
                fw.op("act", lambda e: e.copy(out=cdT.t[0:32, 256:384], in_=pT.t[0:32, 256:384]), reads=[pT], writes=[cdT])
            fw.op("act", lambda e: e.copy(out=cdT.t[:, 0:256], in_=pT.t[:, 0:256]), reads=[pT], writes=[cdT])
            fw.op("dve", lambda e: e.tensor_tensor(out=kk.t[:], in0=k_ap, in1=kkb.t[:], op=ALU.mult),
                  reads=[sh, kkb], writes=[kk])
            fw.op("dve", lambda e: e.tensor_tensor(out=tmp.t[:], in0=kk.t[:], in1=kk.t[:], op=ALU.mult),
                  reads=[kk], writes=[tmp])
            fw.op("dve", lambda e: e.tensor_reduce(out=ss16.t[:], in_=tmp.t[:].rearrange("p (h k) -> p h k", h=16),
                                                   axis=AX.X, op=ALU.add), reads=[tmp], writes=[ss16])
            fw.op("dve", lambda e: e.tensor_scalar(out=ss16.t[:], in0=ss16.t[:], scalar1=1e-12, scalar2=None,
                                                   op0=ALU.add), reads=[ss16], writes=[ss16])
            fw.op("act", lambda e: e.activation(out=ss16.t[:], in_=ss16.t[:], func=AF.Sqrt), reads=[ss16], writes=[ss16])
            fw.op("dve", lambda e: e.reciprocal(out=ss16.t[:], in_=ss16.t[:]), reads=[ss16], writes=[ss16])
            fw.op("dve", lambda e: e.tensor_scalar(out=ss16.t[:], in0=ss16.t[:], scalar1=-1.0, scalar2=None,
                                                   op0=ALU.mult), reads=[ss16], writes=[ss16])
            fw.op("dve", lambda e: e.tensor_tensor(
                out=a_o.t[:].rearrange("p (h k) -> p h k", h=16), in0=kk.t[:].rearrange("p (h k) -> p h k", h=16),
                in1=bc(ss16.t[:].unsqueeze(2), [128, 16, 64]), op=ALU.mult),
                reads=[kk, ss16], writes=[a_o])
            fw.op("dve", lambda e: e.tensor_tensor(out=rk.t[:], in0=r_ap, in1=rkb.t[:], op=ALU.mult),
                  reads=[sh, rkb], writes=[rk])
            if l > 0:
                for hb in range(2):
                    fw.op("pe", lambda e: e.matmul(pvr.t[:, hb * 512:(hb + 1) * 512], lhsT=cdT.t[0:32, 256:384],
                                                   rhs=wvr.t[0:32, hb * 512:(hb + 1) * 512], start=True, stop=True),
                          reads=[cdT, wvr], writes=[pvr])
                fw.op("dve", lambda e: e.tensor_tensor(out=vg.t[:], in0=pvr.t[:], in1=vrb.t[:], op=ALU.add),
                      reads=[pvr, vrb], writes=[vg])
                fw.op("act", lambda e: e.activation(out=vg.t[:], in_=vg.t[:], func=AF.Sigmoid), reads=[vg], writes=[vg])
                fw.op("pool", lambda e: e.tensor_tensor(out=vf.t[:], in0=vf.t[:], in1=v_ap, op=ALU.subtract),
                      reads=[vf, sh], writes=[vf])
                fw.op("pool", lambda e: e.tensor_tensor(out=vf.t[:], in0=vf.t[:], in1=vg.t[:], op=ALU.mult),
                      reads=[vf, vg], writes=[vf])
                fw.op("pool", lambda e: e.tensor_tensor(out=v_o.t[:], in0=vf.t[:], in1=v_ap, op=ALU.add),
                      reads=[vf, sh], writes=[v_o])
                vbuf, vap = v_o, v_o.t[:]
            else:
                vbuf, vap = sh, v_ap
            for d in range(2):
                for hb in range(2):
                    cs = slice(hb * 512, (hb + 1) * 512)
                    fw.op("pe", lambda e: e.matmul(pdec.t[:, cs], lhsT=cdT.t[d * 64:(d + 1) * 64, 0:128],
                                                   rhs=wdec.t[d * 64:(d + 1) * 64, cs], start=True, stop=True),
                          reads=[cdT, wdec], writes=[pdec])
                    fw.op("pe", lambda e: e.matmul(picl.t[:, cs], lhsT=cdT.t[d * 64:(d + 1) * 64, 128:256],
                                                   rhs=wicl.t[d * 64:(d + 1) * 64, cs], start=True, stop=True),
                          reads=[cdT, wicl], writes=[picl])
                fw.op("dve", lambda e: e.tensor_tensor(out=xd.t[:], in0=pdec.t[:], in1=decb[d].t[:], op=ALU.add),
                      reads=[pdec, decb[d]], writes=[xd])
                fw.op("act", lambda e: e.activation(out=xd.t[:], in_=xd.t[:], func=AF.Sigmoid), reads=[xd], writes=[xd])
                fw.op("act", lambda e: e.mul(out=w_o[d].t[:], in_=xd.t[:], mul=-math.exp(-0.5)),
                      reads=[xd], writes=[w_o[d]])
                fw.op("dve", lambda e: e.tensor_tensor(out=al.t[:], in0=picl.t[:], in1=iclb[d].t[:], op=ALU.add),
                      reads=[picl, iclb[d]], writes=[al])
                fw.op("act", lambda e: e.activation(out=al.t[:], in_=al.t[:], func=AF.Sigmoid), reads=[al], writes=[al])
                fw.op("dve", lambda e: e.scalar_tensor_tensor(out=b_o[d].t[:], in0=a_o.t[:], scalar=-1.0, in1=al.t[:],
                                                               op0=ALU.mult, op1=ALU.mult),
                      reads=[a_o, al], writes=[b_o[d]])
                fw.op("dve", lambda e: e.scalar_tensor_tensor(out=tmp.t[:], in0=al.t[:], scalar=-1.0, in1=kab.t[:],
                                                              op0=ALU.add, op1=ALU.mult),
                      reads=[al, kab], writes=[tmp])
                fw.op("dve", lambda e: e.scalar_tensor_tensor(out=kd_o[d].t[:], in0=tmp.t[:], scalar=1.0, in1=k_ap,
                                                              op0=ALU.add, op1=ALU.mult),
                      reads=[tmp, sh], writes=[kd_o[d]])
                fw.op("dve", lambda e: e.tensor_tensor(out=tmp.t[:], in0=rk.t[:], in1=kd_o[d].t[:], op=ALU.mult),
                      reads=[rk, kd_o[d]], writes=[tmp])
                fw.op("dve", lambda e: e.tensor_reduce(out=sd[d].t[:], in_=tmp.t[:].rearrange("p (h k) -> p h k", h=16),
                                                       axis=AX.X, op=ALU.add), reads=[tmp], writes=[sd[d]])
                fw.dma(s_w[d].ap()[rows, :], w_o[d].t[:], w_o[d], False)
                fw.dma(s_b[d].ap()[rows, :], b_o[d].t[:], b_o[d], False)
                fw.dma(s_k[d].ap()[rows, :], kd_o[d].t[:], kd_o[d], False)
            fw.op("dve", lambda e: e.tensor_tensor(out=sd[0].t[:], in0=sd[0].t[:], in1=sd[1].t[:], op=ALU.add),
                  reads=[sd[0], sd[1]], writes=[sd[0]])
            fw.op("dve", lambda e: e.tensor_tensor(
                out=bon_o.t[:].rearrange("p (h k) -> p h k", h=16), in0=vap.rearrange("p (h k) -> p h k", h=16),
                in1=bc(sd[0].t[:].unsqueeze(2), [128, 16, 64]), op=ALU.mult), reads=[vbuf, sd[0]], writes=[bon_o])
            fw.dma(bonus.ap()[rows, :], bon_o.t[:], bon_o, False)
            fw.dma(s_r.ap()[rows, :], r_ap, sh, False)
            fw.dma(s_a.ap()[rows, :], a_o.t[:], a_o, False)
            fw.dma(s_v.ap()[rows, :], vap, vbuf, False)
            if l == 0:
                fw.dma(vfirst.ap()[rows, :], vap, vbuf, False)
        fw.end_phase()

        fw.begin_phase("P3b")
        dif = fw.sb("dif", [128, 128])
        fw.op("pool", lambda e: e.iota(dif.t[:], pattern=[[1, 128]], base=0, channel_multiplier=-1,
                                       allow_small_or_imprecise_dtypes=True), writes=[dif])
        pcol = fw.sb("pcol", [128, 128])
        fw.op("pool", lambda e: e.iota(pcol.t[:], pattern=[[0, 128]], base=0, channel_multiplier=1,
                                       allow_small_or_imprecise_dtypes=True), writes=[pcol])
        jrow = fw.sb("jrow", [128, 128])
        fw.op("pool", lambda e: e.iota(jrow.t[:], pattern=[[1, 128]], base=0, channel_multiplier=0,
                                       allow_small_or_imprecise_dtypes=True), writes=[jrow])
        fw.op("dve", lambda e: e.tensor_scalar(out=pcol.t[:], in0=pcol.t[:], scalar1=64.0, scalar2=None, op0=ALU.is_ge),
              reads=[pcol], writes=[pcol])
        fw.op("dve", lambda e: e.tensor_scalar(out=jrow.t[:], in0=jrow.t[:], scalar1=64.0, scalar2=None, op0=ALU.is_ge),
              reads=[jrow], writes=[jrow])
        same = fw.sb("same", [128, 128])
        fw.op("dve", lambda e: e.tensor_tensor(out=same.t[:], in0=pcol.t[:], in1=jrow.t[:], op=ALU.is_equal),
              reads=[pcol, jrow], writes=[same])
        tri = {}
        for nm, cmp in (("A", ALU.is_ge), ("B", ALU.is_lt), ("C", ALU.is_le), ("D", ALU.is_gt)):
            tri[nm] = fw.sb("tri" + nm, [128, 128])
            fw.op("dve", lambda e: e.tensor_scalar(out=tri[nm].t[:], in0=dif.t[:], scalar1=0.0, scalar2=None, op0=cmp),
                  reads=[dif], writes=[tri[nm]])
            fw.op("dve", lambda e: e.tensor_tensor(out=tri[nm].t[:], in0=tri[nm].t[:], in1=same.t[:], op=ALU.mult),
                  reads=[tri[nm], same], writes=[tri[nm]])
        ind = fw.sb("ind", [128, 2])
        fw.op("dve", lambda e: e.tensor_copy(out=ind.t[:, 1:2], in_=pcol.t[:, 0:1]), reads=[pcol], writes=[ind])
        fw.op("dve", lambda e: e.tensor_scalar(out=ind.t[:, 0:1], in0=pcol.t[:, 0:1], scalar1=-1.0, scalar2=1.0,
                                               op0=ALU.mult, op1=ALU.add), reads=[pcol], writes=[ind])
        elcs = [fw.sb(f"elcs{d}", [128, 8, 64]) for d in range(2)]
        LD = {nm: [fw.sb(f"ld{nm}{i}", [128, D]) for i in range(2)] for nm in ("r", "a", "v")}
        LDd = {nm: [fw.sb(f"ld{nm}{i}", [128, D]) for i in range(2)] for nm in ("w", "b", "k")}
        pLin = fw.ps("pLin", [128, D])
        pLsf = fw.ps("pLsf", [128, D])
        pel = fw.ps("pel", [128, 512])
        ptr3 = [fw.ps(f"ptr3{i}", [128, 8, 128], BF16) for i in range(2)]
        eLin = fw.sb("eLin", [128, D])
        emL = fw.sb("emL", [128, D])
        eLx = fw.sb("eLx", [128, D])
        eLsf = fw.sb("eLsf", [128, D])
        OB = {nm: fw.sb("ob" + nm, [128, D], BF16) for nm in ("Ab", "Rb", "Bh", "Kh", "Bt", "Kt", "Vb")}
        XT = [fw.sb(f"XT{i}", [128, 8, 128], BF16) for i in range(2)]
        assert fw.sb_bytes < 190 * 1024, fw.sb_bytes
        xcnt = 0
        for ti in range(NT):
            rows = slice(ti * 128, (ti + 1) * 128)
            pb_ = ti % 2
            r_t, a_t, v_t = LD["r"][pb_], LD["a"][pb_], LD["v"][pb_]
            fw.dma(r_t.t[:], s_r.ap()[rows, :], r_t, True)
            fw.dma(a_t.t[:], s_a.ap()[rows, :], a_t, True)
            fw.dma(v_t.t[:], s_v.ap()[rows, :], v_t, True)
            fw.op("act", lambda e: e.copy(out=OB["Vb"].t[:], in_=v_t.t[:]), reads=[v_t], writes=[OB["Vb"]])
            fw.dma(tm_v.ap()[rows, :], OB["Vb"].t[:], OB["Vb"], False)
            for d in range(2):
                w_t, b_t, k_t = LDd["w"][d], LDd["b"][d], LDd["k"][d]
                fw.dma(w_t.t[:], s_w[d].ap()[rows, :], w_t, True)
                fw.dma(b_t.t[:], s_b[d].ap()[rows, :], b_t, True)
                fw.dma(k_t.t[:], s_k[d].ap()[rows, :], k_t, True)
                tin, tsf = (tri["A"], tri["B"]) if d == 0 else (tri["C"], tri["D"])
                for hb in range(2):
                    cs = slice(hb * 512, (hb + 1) * 512)
                    fw.op("pe", lambda e: e.matmul(pLin.t[:, cs], lhsT=tin.t[:], rhs=w_t.t[:, cs], start=True, stop=True),
                          reads=[tin, w_t], writes=[pLin])
                    fw.op("pe", lambda e: e.matmul(pLsf.t[:, cs], lhsT=tsf.t[:], rhs=w_t.t[:, cs], start=True, stop=True),
                          reads=[tsf, w_t], writes=[pLsf])
                for fc in range(8):
                    fw.op("pe", lambda e: e.matmul(pel.t[:, fc * 2:fc * 2 + 2], lhsT=w_t.t[:, fc * 128:(fc + 1) * 128],
                                                   rhs=ind.t[:], start=(fc == 0), stop=(fc == 7), skip_group_check=True),
                          reads=[w_t, ind], writes=[pel])
                fw.op("act", lambda e: e.activation(out=elcs[d].t[:, :, 2 * ti:2 * ti + 2],
                                                    in_=pel.t[:, 0:16].rearrange("p (f c) -> p f c", c=2), func=AF.Exp),
                      reads=[pel], writes=[elcs[d]])
                fw.op("act", lambda e: e.activation(out=eLin.t[:], in_=pLin.t[:], func=AF.Exp), reads=[pLin], writes=[eLin])
                fw.op("act", lambda e: e.activation(out=emL.t[:], in_=pLin.t[:], func=AF.Exp, scale=-1.0),
                      reads=[pLin], writes=[emL])
                fw.op("dve", lambda e: e.tensor_tensor(out=eLx.t[:], in0=pLin.t[:], in1=w_t.t[:], op=ALU.subtract),
                      reads=[pLin, w_t], writes=[eLx])
                fw.op("act", lambda e: e.activation(out=eLx.t[:], in_=eLx.t[:], func=AF.Exp), reads=[eLx], writes=[eLx])
                fw.op("act", lambda e: e.activation(out=eLsf.t[:], in_=pLsf.t[:], func=AF.Exp), reads=[pLsf], writes=[eLsf])
                for (onm, x_, e_) in (("Ab", a_t, eLx), ("Rb", r_t, eLin), ("Bh", b_t, emL), ("Kh", k_t, emL),
                                      ("Bt", b_t, eLsf), ("Kt", k_t, eLsf)):
                    eng = "dve" if onm in ("Ab", "Rb", "Bh", "Kh") else "pool"
                    fw.op(eng, lambda e: e.tensor_tensor(out=OB[onm].t[:], in0=x_.t[:], in1=e_.t[:], op=ALU.mult),
                          reads=[x_, e_], writes=[OB[onm]])
                fw.dma(tm_b[d].ap()[rows, :], OB["Bt"].t[:], OB["Bt"], False)
                fw.dma(tm_k[d].ap()[rows, :], OB["Kt"].t[:], OB["Kt"], False)
                for (onm, dst) in (("Ab", fm_a[d]), ("Rb", fm_r[d]), ("Bh", fm_b[d]), ("Kh", fm_k[d])):
                    pt = ptr3[xcnt % 2]
                    xt_ = XT[xcnt % 2]
                    xcnt += 1
                    for kc in range(8):
                        fw.op("pe", lambda e: e.transpose(out=pt.t[:, kc, :], in_=OB[onm].t[:, kc * 128:(kc + 1) * 128],
                                                          identity=ident.t[:]), reads=[OB[onm], ident], writes=[pt])
                    fw.op("act", lambda e: e.copy(out=xt_.t[:], in_=pt.t[:]), reads=[pt], writes=[xt_])
                    fw.dma(bass.AP(dst, ti * 128, [[NTOK, 128], [128 * NTOK, 8], [1, 128]]), xt_.t[:], xt_, False)
        for d in range(2):
            fw.dma(elc[d].ap(), elcs[d].t[:].rearrange("p f c -> p (f c)"), elcs[d], False)
        fw.end_phase()

        fw.begin_phase("P4")
        CH = 64
        NCH = SEQ // CH
        dif64 = fw.sb("dif64", [64, 64])
        fw.op("pool", lambda e: e.iota(dif64.t[:], pattern=[[1, 64]], base=0, channel_multiplier=-1,
                                       allow_small_or_imprecise_dtypes=True), writes=[dif64])
        mk = {}
        for nm, cmp in (("LT", ALU.is_gt), ("LE", ALU.is_ge), ("GT", ALU.is_lt), ("GE", ALU.is_le)):
            mk[nm] = fw.sb("mk" + nm, [64, 64])
            fw.op("dve", lambda e: e.tensor_scalar(out=mk[nm].t[:], in0=dif64.t[:], scalar1=0.0, scalar2=None, op0=cmp),
                  reads=[dif64], writes=[mk[nm]])
        I2 = fw.sb("I2", [64, 2, 64])
        for h in range(2):
            fw.op("dve", lambda e: e.tensor_scalar(out=I2.t[:, h, :], in0=dif64.t[:], scalar1=0.0, scalar2=None,
                                                   op0=ALU.is_equal), reads=[dif64], writes=[I2])
        mbc = [fw.sb(f"mbc{d}", [64, 128]) for d in range(2)]
        fw.op("dve", lambda e: e.tensor_copy(out=mbc[0].t[:, 0:64], in_=mk["LT"].t[:]), reads=[mk["LT"]], writes=[mbc[0]])
        fw.op("dve", lambda e: e.tensor_copy(out=mbc[0].t[:, 64:128], in_=mk["LE"].t[:]), reads=[mk["LE"]], writes=[mbc[0]])
        fw.op("dve", lambda e: e.tensor_copy(out=mbc[1].t[:, 0:64], in_=mk["GT"].t[:]), reads=[mk["GT"]], writes=[mbc[1]])
        fw.op("dve", lambda e: e.tensor_copy(out=mbc[1].t[:, 64:128], in_=mk["GE"].t[:]), reads=[mk["GE"]], writes=[mbc[1]])
        mka = [mk["GT"], mk["LT"]]

        AR = [fw.sb(f"AR{i}", [64, 2, NCH, 2, CH], BF16) for i in range(2)]
        BhT = [fw.sb("BhT0", [64, 2, SEQ], BF16)] * 2
        KhT = [fw.sb("KhT0", [64, 2, SEQ], BF16)] * 2
        Bt_ = [fw.sb(f"Btm{i}", [64, NCH, 128], BF16) for i in range(2)]
        Kt_ = [fw.sb(f"Ktm{i}", [64, NCH, 128], BF16) for i in range(2)]
        Vt_ = [fw.sb(f"Vtm{i}", [64, NCH, 128], BF16) for i in range(2)]
        EL = [fw.sb(f"EL{i}", [64, 2, NCH]) for i in range(2)]
        TT = [fw.sb(f"TT{i}", [64, NCH, 2, CH], BF16) for i in range(2)]
        AKRK = [fw.sb(f"AKRK{i}", [64, NCH, 2, 2 * CH], BF16) for i in range(2)]
        ARB = [fw.sb(f"ARB{i}", [64, NCH, 2, CH], BF16) for i in range(2)]
        NLANE = 2
        WL = [[fw.sb(f"W{ln}{i}", [64, 2, 3 * CH]) for i in range(2)] for ln in range(NLANE)]
        Gb = fw.sb("Gb", [64, 128], BF16)
        Ub = fw.sb("Ub", [64, 128], BF16)
        Sf = fw.sb("Sf", [64, 2, 64])
        Sbf = fw.sb("Sbf", [64, 2, 64], BF16)
        Yst = [fw.sb(f"Yst{i}", [64, 8, 128]) for i in range(2)]
        ps_scL = [fw.ps(f"ps_sc{ln}", [128, 512]) for ln in range(NLANE)]
        ps_cL = [fw.ps(f"ps_c{ln}", [128, 512]) for ln in range(NLANE)]
        ps_invL = [fw.ps(f"ps_inv{ln}", [128, 512]) for ln in range(NLANE)]
        ps_gu = fw.ps("ps_gu", [128, 512])
        ps_ys = fw.ps("ps_ys", [128, 512])
        assert fw.sb_bytes < 186 * 1024, fw.sb_bytes

        scans = [(s, d, hp) for s in range(NSEQ) for d in range(2) for hp in range(8)]

        def load_scan(idx):
            s, d, hp = scans[idx]
            pb_ = idx % 2
            tok0 = s * SEQ
            for h in range(2):
                row0 = (hp * 128 + h * 64) * NTOK + tok0
                for which, src in ((0, fm_a[d]), (1, fm_r[d])):
                    for q4 in range(4):
                        fw.dma(AR[pb_].t[:, h, q4 * 8:(q4 + 1) * 8, which, :],
                               bass.AP(src, row0 + q4 * 8 * CH, [[NTOK, 64], [CH, 8], [1, CH]]), AR[pb_], True)
                fw.dma(BhT[pb_].t[:, h, :], bass.AP(fm_b[d], row0, [[NTOK, 64], [1, SEQ]]), BhT[pb_], True)
                fw.dma(KhT[pb_].t[:, h, :], bass.AP(fm_k[d], row0, [[NTOK, 64], [1, SEQ]]), KhT[pb_], True)
                fw.dma(EL[pb_].t[:, h, :], bass.AP(elc[d], h * 64 * 512 + hp * 64 + s * NCH, [[512, 64], [1, NCH]]),
                       EL[pb_], True)
            for dst, src in ((Bt_[pb_], tm_b[d]), (Kt_[pb_], tm_k[d]), (Vt_[pb_], tm_v)):
                for q4 in range(4):
                    fw.dma(dst.t[:, q4 * 8:(q4 + 1) * 8, :],
                           bass.AP(src, (tok0 + q4 * 8 * CH) * D + hp * 128, [[D, CH], [CH * D, 8], [1, 128]]), dst, True)

        def gen_A(idx, c, lane):
            W = WL[lane]
            ps_sc, ps_c, ps_inv = ps_scL[lane], ps_cL[lane], ps_invL[lane]
            s, d, hp = scans[idx]
            pb_ = idx % 2
            t0 = c * CH
            ar = AR[pb_]
            for h in range(2):
                ar_h = ar.t[:, h, c, :, :].rearrange("p a b -> p (a b)")
                fw.op("pe", lambda e: e.matmul(ps_sc.t[0:64, h * 64:(h + 1) * 64], lhsT=ar.t[:, h, c, 0, :],
                                               rhs=BhT[pb_].t[:, h, t0:t0 + CH], start=(h == 0), stop=False,
                                               skip_group_check=True), reads=[ar, BhT[pb_]], writes=[ps_sc])
                fw.op("pe", lambda e: e.matmul(ps_sc.t[0:64, 128 + h * 128:128 + (h + 1) * 128],
                                               lhsT=BhT[pb_].t[:, h, t0:t0 + CH], rhs=ar_h, start=False, stop=(h == 1),
                                               skip_group_check=True), reads=[ar, BhT[pb_]], writes=[ps_sc])
                fw.op("pe", lambda e: e.matmul(ps_c.t[0:64, h * 128:(h + 1) * 128], lhsT=KhT[pb_].t[:, h, t0:t0 + CH],
                                               rhs=ar_h, start=(h == 0), stop=(h == 1), skip_group_check=True),
                      reads=[ar, KhT[pb_]], writes=[ps_c])
            yield
            w0 = W[0]
            sc_n = ps_sc.t[0:64, 0:128].rearrange("p (h x) -> p h x", h=2)
            sc_b = ps_sc.t[0:64, 128:384].rearrange("p (h x) -> p h x", h=2)
            sc_c = ps_c.t[0:64, 0:256].rearrange("p (h x) -> p h x", h=2)
            fw.op("dve", lambda e: e.tensor_tensor(out=w0.t[:, :, 128:192], in0=sc_n, in1=bc(mka[d].t[:].unsqueeze(1), [64, 2, 64]),
                                                   op=ALU.mult), reads=[ps_sc, mka[d]], writes=[w0])
            fw.op("dve", lambda e: e.tensor_tensor(out=w0.t[:, :, 0:64], in0=sc_b[:, :, 0:64],
                                                   in1=bc(mbc[d].t[:, 0:64].unsqueeze(1), [64, 2, 64]), op=ALU.mult),
                  reads=[ps_sc, mbc[d]], writes=[w0])
            fw.op("dve", lambda e: e.tensor_tensor(out=ARB[pb_].t[:, c, :, :], in0=sc_b[:, :, 64:128],
                                                   in1=bc(mbc[d].t[:, 64:128].unsqueeze(1), [64, 2, 64]), op=ALU.mult),
                  reads=[ps_sc, mbc[d]], writes=[ARB[pb_]])
            fw.op("dve", lambda e: e.tensor_tensor(out=AKRK[pb_].t[:, c, :, :], in0=sc_c,
                                                   in1=bc(mbc[d].t[:].unsqueeze(1), [64, 2, 128]), op=ALU.mult),
                  reads=[ps_c, mbc[d]], writes=[AKRK[pb_]])
            fw.op("pool", lambda e: e.tensor_copy(out=w0.t[:, :, 64:128], in_=I2.t[:]), reads=[I2], writes=[w0])
            yield
            inv = ps_inv.t[0:64, 0:384].rearrange("p (h x) -> p h x", h=2)
            inv4 = inv.rearrange("p h (b x) -> p h b x", b=3)
            for j in range(5):
                wj, wn = W[j % 2], W[(j + 1) % 2]
                for h in range(2):
                    fw.op("pe", lambda e: e.matmul(inv[:, h, 0:128], lhsT=wj.t[:, h, 128:192], rhs=wj.t[:, h, 0:128],
                                                   start=(h == 0), stop=False, skip_group_check=True),
                          reads=[wj], writes=[ps_inv])
                    fw.op("pe", lambda e: e.matmul(inv[:, h, 128:192], lhsT=wj.t[:, h, 0:64], rhs=wj.t[:, h, 128:192],
                                                   start=False, stop=(h == 1), skip_group_check=True),
                          reads=[wj], writes=[ps_inv])
                yield
                if j < 4:
                    fw.op("act", lambda e: e.copy(out=wn.t[:].rearrange("p h (b x) -> p h b x", b=3)[:, :, 0:3:2, :],
                                                  in_=inv4[:, :, 0:3:2, :]), reads=[ps_inv], writes=[wn])
                else:
                    fw.op("act", lambda e: e.copy(out=wn.t[:, :, 128:192], in_=inv[:, :, 128:192]), reads=[ps_inv], writes=[wn])
                fw.op("dve", lambda e: e.tensor_tensor(out=wn.t[:, :, 64:128], in0=inv[:, :, 64:128],
                                                       in1=wj.t[:, :, 64:128], op=ALU.add),
                      reads=[ps_inv, wj], writes=[wn])
                yield
            w5 = W[1]
            for h in range(2):
                fw.op("pe", lambda e: e.matmul(inv[:, h, 0:64], lhsT=w5.t[:, h, 128:192], rhs=w5.t[:, h, 64:128],
                                               start=(h == 0), stop=(h == 1), skip_group_check=True),
                      reads=[w5], writes=[ps_inv])
            yield
            fw.op("dve", lambda e: e.tensor_tensor(out=TT[pb_].t[:, c, :, :], in0=inv[:, :, 0:64], in1=w5.t[:, :, 64:128],
                                                   op=ALU.add), reads=[ps_inv, w5], writes=[TT[pb_]])

        def gen_B(idx, k):
            s, d, hp = scans[idx]
            pb_ = idx % 2
            c = k if d == 0 else NCH - 1 - k
            ar, vt, bt, kt = AR[pb_], Vt_[pb_], Bt_[pb_], Kt_[pb_]
            if k == 0:
                fw.op("pool", lambda e: e.memset(Sf.t[:], 0.0), writes=[Sf])
                fw.op("pool", lambda e: e.memset(Sbf.t[:], 0.0), writes=[Sbf])
            g_ = ps_gu.t[0:64, 0:128]
            for h in range(2):
                cs = slice(h * 64, (h + 1) * 64)
                fw.op("pe", lambda e: e.matmul(ps_gu.t[0:64, cs], lhsT=ar.t[:, h, c, 0, :], rhs=Sbf.t[:, h, :],
                                               start=(h == 0), stop=False, skip_group_check=True),
                      reads=[ar, Sbf], writes=[ps_gu])
                fw.op("pe", lambda e: e.matmul(ps_gu.t[0:64, cs], lhsT=AKRK[pb_].t[:, c, h, 0:64], rhs=vt.t[:, c, cs],
                                               start=False, stop=(h == 1), skip_group_check=True),
                      reads=[AKRK[pb_], vt], writes=[ps_gu])
            yield
            fw.op("act", lambda e: e.copy(out=Gb.t[:], in_=g_), reads=[ps_gu], writes=[Gb])
            yield
            for h in range(2):
                cs = slice(h * 64, (h + 1) * 64)
                fw.op("pe", lambda e: e.matmul(ps_gu.t[0:64, 128 + h * 64:128 + (h + 1) * 64], lhsT=TT[pb_].t[:, c, h, :], rhs=Gb.t[:, cs],
                                               start=(h == 0), stop=(h == 1), skip_group_check=True),
                      reads=[TT[pb_], Gb], writes=[ps_gu])
            yield
            fw.op("dve", lambda e: e.tensor_copy(out=Ub.t[:], in_=ps_gu.t[0:64, 128:256]), reads=[ps_gu], writes=[Ub])
            yield
            y_ = ps_ys.t[0:64, 0:128]
            for h in range(2):
                cs = slice(h * 64, (h + 1) * 64)
                fw.op("pe", lambda e: e.matmul(ps_ys.t[0:64, cs], lhsT=ar.t[:, h, c, 1, :], rhs=Sbf.t[:, h, :],
                                               start=(h == 0), stop=False, skip_group_check=True),
                      reads=[ar, Sbf], writes=[ps_ys])
                fw.op("pe", lambda e: e.matmul(ps_ys.t[0:64, cs], lhsT=ARB[pb_].t[:, c, h, :], rhs=Ub.t[:, cs],
                                               start=False, stop=False, skip_group_check=True),
                      reads=[ARB[pb_], Ub], writes=[ps_ys])
                fw.op("pe", lambda e: e.matmul(ps_ys.t[0:64, cs], lhsT=AKRK[pb_].t[:, c, h, 64:128], rhs=vt.t[:, c, cs],
                                               start=False, stop=(h == 1), skip_group_check=True),
                      reads=[AKRK[pb_], vt], writes=[ps_ys])
            yst = Yst[(k // 8) % 2]
            for h in range(2):
                cs = slice(h * 64, (h + 1) * 64)
                fw.op("pe", lambda e: e.matmul(ps_ys.t[0:64, 128 + h * 64:128 + (h + 1) * 64], lhsT=kt.t[:, c, cs], rhs=vt.t[:, c, cs],
                                               start=(h == 0), stop=False, skip_group_check=True),
                      reads=[kt, vt], writes=[ps_ys])
                fw.op("pe", lambda e: e.matmul(ps_ys.t[0:64, 128 + h * 64:128 + (h + 1) * 64], lhsT=bt.t[:, c, cs], rhs=Ub.t[:, cs],
                                               start=False, stop=(h == 1), skip_group_check=True),
                      reads=[bt, Ub], writes=[ps_ys])
            yield
            fw.op("act", lambda e: e.copy(out=yst.t[:, c % 8, :], in_=y_), reads=[ps_ys], writes=[yst])
            for h in range(2):
                cs = slice(h * 64, (h + 1) * 64)
                fw.op("dve", lambda e: e.scalar_tensor_tensor(out=Sf.t[:, h, :], in0=Sf.t[:, h, :],
                                                              scalar=EL[pb_].t[:, h, c:c + 1], in1=ps_ys.t[0:64, 128 + h * 64:128 + (h + 1) * 64],
                                                              op0=ALU.mult, op1=ALU.add),
                      reads=[Sf, EL[pb_], ps_ys], writes=[Sf])
            fw.op("act", lambda e: e.copy(out=Sbf.t[:], in_=Sf.t[:]), reads=[Sf], writes=[Sbf])
            if k % 8 == 7:
                cb0 = (c // 8) * 8
                fw.dma(bass.AP(ysc[d], (s * SEQ + cb0 * CH) * D + hp * 128, [[D, CH], [CH * D, 8], [1, 128]]),
                       yst.t[:], yst, False)
            yield

        nsc = len(scans)

        def run_streams(a_idx, b_idx):
            a_chunks = [list(range(ln, NCH, NLANE)) for ln in range(NLANE)] if a_idx is not None else [[] for _ in range(NLANE)]
            b_chunks = list(range(NCH)) if b_idx is not None else []
            gens = [None] * (NLANE + 1)
            while True:
                alive = False
                for gi in range(NLANE + 1):
                    if gens[gi] is None:
                        if gi < NLANE and a_chunks[gi]:
                            gens[gi] = gen_A(a_idx, a_chunks[gi].pop(0), gi)
                        elif gi == NLANE and b_chunks:
                            gens[gi] = gen_B(b_idx, b_chunks.pop(0))
                    if gens[gi] is not None:
                        alive = True
                        try:
                            next(gens[gi])
                        except StopIteration:
                            gens[gi] = None
                if not alive:
                    break

        load_scan(0)
        run_streams(0, None)
        for idx in range(nsc):
            if idx + 1 < nsc:
                load_scan(idx + 1)
            run_streams(idx + 1 if idx + 1 < nsc else None, idx)
        fw.end_phase()

        fw.begin_phase("P6")
        d0 = fw.sb("d0", [128, 3968])
        fw.op("pool", lambda e: e.iota(d0.t[:], pattern=[[1, 3968]], base=-1920, channel_multiplier=-1,
                                       allow_small_or_imprecise_dtypes=True), writes=[d0])
        fw.op("dve", lambda e: e.scalar_tensor_tensor(out=d0.t[:], in0=d0.t[:], scalar=-1.0, in1=d0.t[:],
                                                      op0=ALU.mult, op1=ALU.max), reads=[d0], writes=[d0])
        qT = fw.sb("qT", [128, 4, SEQ], BF16)
        kT = fw.sb("kT", [128, 4, SEQ], BF16)
        vball = fw.sb("vball", [128, 16, D], BF16)
        Eh = [fw.sb(f"Eh{i}", [128, 3968]) for i in range(2)]
        lnb = fw.sb("lnb", [128, 1])
        fw.op("pool", lambda e: e.memset(lnb.t[:], math.log(0.125)), writes=[lnb])
        qin = [fw.sb(f"qin{i}", [128, 512]) for i in range(2)]
        kin = [fw.sb(f"kin{i}", [128, 512]) for i in range(2)]
        vin = [fw.sb(f"vin{i}", [128, D]) for i in range(2)]
        rt1 = fw.sb("rt1", [128, 8, 32])
        rt2 = fw.sb("rt2", [128, 8, 32])
        qrb = fw.sb("qrb", [128, 512], BF16)
        krb = fw.sb("krb", [128, 512], BF16)
        ptq = fw.ps("ptq", [128, 8, 128], BF16)
        ptk = fw.ps("ptk", [128, 8, 128], BF16)
        psc = [fw.ps(f"psc{i}", [128, 512]) for i in range(3)]
        pyr = [fw.ps(f"pyr{i}", [128, 512]) for i in range(2)]
        At = [fw.sb(f"At{i}", [128, 512], BF16) for i in range(3)]
        yst = [fw.sb(f"yst{i}", [128, 512]) for i in range(2)]
        assert fw.sb_bytes < 190 * 1024, fw.sb_bytes
        ecnt = 0
        for s in range(NSEQ):
            for j in range(16):
                b = j % 2
                rows = slice(s * SEQ + j * 128, s * SEQ + (j + 1) * 128)
                fw.dma(qin[b].t[:], proj.ap()[rows, C_QB:C_QB + 512], qin[b], True)
                fw.dma(kin[b].t[:], proj.ap()[rows, C_KB:C_KB + 512], kin[b], True)
                fw.dma(vin[b].t[:], proj.ap()[rows, C_VB:C_VB + D], vin[b], True)
                cosb = bc(cos_t.t[:, j, :].unsqueeze(1), [128, 8, 32])
                sinb = bc(sin_t.t[:, j, :].unsqueeze(1), [128, 8, 32])
                for (src, dst) in ((qin[b], qrb), (kin[b], krb)):
                    s3 = src.t[:].rearrange("p (h k) -> p h k", h=8)
                    d3 = dst.t[:].rearrange("p (h k) -> p h k", h=8)
                    x1, x2 = s3[:, :, 0:32], s3[:, :, 32:64]
                    fw.op("dve", lambda e: e.tensor_tensor(out=rt1.t[:], in0=x1, in1=cosb, op=ALU.mult),
                          reads=[src, cos_t], writes=[rt1])
                    fw.op("dve", lambda e: e.tensor_tensor(out=rt2.t[:], in0=x2, in1=sinb, op=ALU.mult),
                          reads=[src, sin_t], writes=[rt2])
                    fw.op("dve", lambda e: e.tensor_tensor(out=d3[:, :, 0:32], in0=rt1.t[:], in1=rt2.t[:], op=ALU.subtract),
                          reads=[rt1, rt2], writes=[dst])
                    fw.op("dve", lambda e: e.tensor_tensor(out=rt1.t[:], in0=x1, in1=sinb, op=ALU.mult),
                          reads=[src, sin_t], writes=[rt1])
                    fw.op("dve", lambda e: e.tensor_tensor(out=rt2.t[:], in0=x2, in1=cosb, op=ALU.mult),
                          reads=[src, cos_t], writes=[rt2])
                    fw.op("dve", lambda e: e.tensor_tensor(out=d3[:, :, 32:64], in0=rt1.t[:], in1=rt2.t[:], op=ALU.add),
                          reads=[rt1, rt2], writes=[dst])
                fw.op("act", lambda e: e.copy(out=vball.t[:, j, :], in_=vin[b].t[:]), reads=[vin[b]], writes=[vball])
                for (src, pt, dstT) in ((qrb, ptq, qT), (krb, ptk, kT)):
                    for hp in range(4):
                        fw.op("pe", lambda e: e.transpose(out=pt.t[:, hp, :], in_=src.t[:, hp * 128:(hp + 1) * 128],
                                                          identity=ident.t[:]), reads=[src, ident], writes=[pt])
                    fw.op("act", lambda e: e.copy(out=dstT.t[:, :, j * 128:(j + 1) * 128], in_=pt.t[:, 0:4, :]),
                          reads=[pt], writes=[dstT])
            for h in range(8):
                hp, pb = h // 2, (h % 2) * 64
                lg = math.log1p(-(2.0 ** (-5 - h)))
                E = Eh[ecnt % 2]
                ecnt += 1
                fw.op("act", lambda e: e.activation(out=E.t[:], in_=d0.t[:], func=AF.Exp, scale=lg, bias=lnb.t[:, 0:1]),
                      reads=[d0, lnb], writes=[E])
                for nb in range(4):
                    n0 = nb * 512
                    py = pyr[nb % 2]
                    for mt in range(16):
                        m0 = mt * 128
                        ps_ = psc[mt % 3]
                        at = At[mt % 3]
                        fw.op("pe", lambda e: e.matmul(ps_.t[:], lhsT=kT.t[pb:pb + 64, hp, m0:m0 + 128],
                                                       rhs=qT.t[pb:pb + 64, hp, n0:n0 + 512], start=True, stop=True),
                              reads=[kT, qT], writes=[ps_])
                        c0 = n0 - m0 + 1920
                        fw.op("dve", lambda e: e.tensor_tensor(out=at.t[:], in0=ps_.t[:], in1=E.t[:, c0:c0 + 512],
                                                               op=ALU.mult), reads=[ps_, E], writes=[at])
                        for nt in range(4):
                            fw.op("pe", lambda e: e.matmul(py.t[:, nt * 128:(nt + 1) * 128],
                                                           lhsT=at.t[:, nt * 128:(nt + 1) * 128],
                                                           rhs=vball.t[:, mt, h * 128:(h + 1) * 128],
                                                           start=(mt == 0 and nt == 0), stop=(mt == 15),
                                                           skip_group_check=True),
                                  reads=[at, vball], writes=[py])
                    ys = yst[nb % 2]
                    fw.op("act", lambda e: e.copy(out=ys.t[:], in_=py.t[:]), reads=[py], writes=[ys])
                    fw.dma(bass.AP(yret, (s * SEQ + n0) * D + h * 128, [[D, 128], [128 * D, 4], [1, 128]]),
                           ys.t[:].rearrange("p (n v) -> p n v", n=4), ys, False)
        fw.end_phase()

        fw.begin_phase("P7")
        lgb = fw.sb("lgb", [128, D])
        lbb = fw.sb("lbb", [128, D])
        rgb = fw.sb("rgb", [128, D])
        fw.dma(lgb.t[:], row_bc(lnx_gain, l * D, D), lgb, True)
        fw.dma(lbb.t[:], row_bc(lnx_bias, l * D, D), lbb, True)
        fw.dma(rgb.t[:], row_bc(ret_norm_gain, l * D, D), rgb, True)
        if last:
            fgb = fw.sb("fgb", [128, D])
            fw.dma(fgb.t[:], row_bc(final_gain, 0, D), fgb, True)
        wsg = [fw.sb(f"wsg{i}", [128, 2, D]) for i in range(2)]
        Wb = {}
        wi = 0
        for nm, wt in (("a", w_branch_a), ("b", w_branch_b), ("o", w_out)):
            Wb[nm] = fw.sb("W" + nm, [128, 8, D], BF16)
            for c in range(4):
                sg = wsg[wi % 2]
                wi += 1
                fw.dma(sg.t[:], bass.AP(wt, l * D * D + c * 256 * D, [[D, 128], [128 * D, 2], [1, D]]), sg, True)
                fw.op("pool", lambda e: e.tensor_copy(out=Wb[nm].t[:, 2 * c:2 * c + 2, :], in_=sg.t[:]),
                      reads=[sg], writes=[Wb[nm]])
        names = ("yf", "yb", "bon", "ga", "yr", "gb", "ma", "mb", "xt")
        L = {nm: fw.sb("L" + nm, [128, D]) for nm in names}
        y = fw.sb("y", [128, D])
        sq = fw.sb("sq", [128, D])
        st16 = fw.sb("st16", [128, 16])
        st8 = fw.sb("st8", [128, 8])
        yabf = fw.sb("yabf", [128, D], BF16)
        ybbf = fw.sb("ybbf", [128, D], BF16)
        mgbf = fw.sb("mgbf", [128, D], BF16)
        yaT = fw.sb("yaT", [128, 8, 128], BF16)
        ybT = fw.sb("ybT", [128, 8, 128], BF16)
        mgT = fw.sb("mgT", [128, 8, 128], BF16)
        ptp = fw.ps("ptp", [128, 8, 128], BF16)
        pa = fw.ps("pa", [128, D])
        pbb = fw.ps("pbb", [128, D])
        po = fw.ps("po", [128, D])
        xo = fw.sb("xo", [128, D])
        fss = fw.sb("fss", [128, 1])
        assert fw.sb_bytes < 190 * 1024, fw.sb_bytes

        def head_norm(src, nh, hd, eps, stt):
            s3 = src.t[:].rearrange("p (h k) -> p h k", h=nh)
            q3 = sq.t[:].rearrange("p (h k) -> p h k", h=nh)
            fw.op("dve", lambda e: e.tensor_reduce(out=stt.t[:], in_=s3, axis=AX.X, op=ALU.add), reads=[src], writes=[stt])
            fw.op("dve", lambda e: e.tensor_scalar(out=stt.t[:], in0=stt.t[:], scalar1=-1.0 / hd, scalar2=None,
                                                   op0=ALU.mult), reads=[stt], writes=[stt])
            fw.op("dve", lambda e: e.tensor_tensor(out=s3, in0=s3, in1=bc(stt.t[:].unsqueeze(2), [128, nh, hd]),
                                                   op=ALU.add), reads=[src, stt], writes=[src])
            fw.op("pool", lambda e: e.tensor_tensor(out=sq.t[:], in0=src.t[:], in1=src.t[:], op=ALU.mult),
                  reads=[src], writes=[sq])
            fw.op("dve", lambda e: e.tensor_reduce(out=stt.t[:], in_=q3, axis=AX.X, op=ALU.add), reads=[sq], writes=[stt])
            rstd_from(stt, hd, eps)
            fw.op("dve", lambda e: e.tensor_tensor(out=s3, in0=s3, in1=bc(stt.t[:].unsqueeze(2), [128, nh, hd]),
                                                   op=ALU.mult), reads=[src, stt], writes=[src])

        for ti in range(NT):
            rows = slice(ti * 128, (ti + 1) * 128)
            fw.dma(L["yf"].t[:], ysc[0].ap()[rows, :], L["yf"], True)
            fw.dma(L["yb"].t[:], ysc[1].ap()[rows, :], L["yb"], True)
            fw.dma(L["bon"].t[:], bonus.ap()[rows, :], L["bon"], True)
            fw.dma(L["ga"].t[:], proj.ap()[rows, C_GA:C_GA + D], L["ga"], True)
            fw.dma(L["yr"].t[:], yret.ap()[rows, :], L["yr"], True)
            fw.dma(L["gb"].t[:], proj.ap()[rows, C_GB:C_GB + D], L["gb"], True)
            fw.dma(L["ma"].t[:], proj.ap()[rows, C_MA:C_MA + D], L["ma"], True)
            fw.dma(L["mb"].t[:], proj.ap()[rows, C_MB:C_MB + D], L["mb"], True)
            fw.dma(L["xt"].t[:], xcur.ap()[rows, :], L["xt"], True)
            fw.op("dve", lambda e: e.tensor_tensor(out=y.t[:], in0=L["yf"].t[:], in1=L["yb"].t[:], op=ALU.add),
                  reads=[L["yf"], L["yb"]], writes=[y])
            head_norm(y, 16, 64, GN_EPS, st16)
            fw.op("dve", lambda e: e.tensor_tensor(out=y.t[:], in0=y.t[:], in1=lgb.t[:], op=ALU.mult),
                  reads=[y, lgb], writes=[y])
            fw.op("pool", lambda e: e.tensor_tensor(out=L["bon"].t[:], in0=L["bon"].t[:], in1=lbb.t[:], op=ALU.add),
                  reads=[L["bon"], lbb], writes=[L["bon"]])
            fw.op("dve", lambda e: e.tensor_tensor(out=y.t[:], in0=y.t[:], in1=L["bon"].t[:], op=ALU.add),
                  reads=[y, L["bon"]], writes=[y])
            fw.op("act", lambda e: e.activation(out=L["ga"].t[:], in_=L["ga"].t[:], func=AF.Silu),
                  reads=[L["ga"]], writes=[L["ga"]])
            fw.op("dve", lambda e: e.tensor_tensor(out=yabf.t[:], in0=y.t[:], in1=L["ga"].t[:], op=ALU.mult),
                  reads=[y, L["ga"]], writes=[yabf])
            head_norm(L["yr"], 8, 128, NORM_EPS, st8)
            fw.op("act", lambda e: e.activation(out=L["gb"].t[:], in_=L["gb"].t[:], func=AF.Silu),
                  reads=[L["gb"]], writes=[L["gb"]])
            fw.op("pool", lambda e: e.tensor_tensor(out=L["gb"].t[:], in0=L["gb"].t[:], in1=rgb.t[:], op=ALU.mult),
                  reads=[L["gb"], rgb], writes=[L["gb"]])
            fw.op("dve", lambda e: e.tensor_tensor(out=ybbf.t[:], in0=L["yr"].t[:], in1=L["gb"].t[:], op=ALU.mult),
                  reads=[L["yr"], L["gb"]], writes=[ybbf])
            for (src, dT, pacc, W) in ((yabf, yaT, pa, Wb["a"]), (ybbf, ybT, pbb, Wb["b"])):
                for kc in range(8):
                    fw.op("pe", lambda e: e.transpose(out=ptp.t[:, kc, :], in_=src.t[:, kc * 128:(kc + 1) * 128],
                                                      identity=ident.t[:]), reads=[src, ident], writes=[ptp])
                fw.op("act", lambda e: e.copy(out=dT.t[:], in_=ptp.t[:]), reads=[ptp], writes=[dT])
                for hb in range(2):
                    for kc in range(8):
                        fw.op("pe", lambda e: e.matmul(pacc.t[:, hb * 512:(hb + 1) * 512], lhsT=dT.t[:, kc, :],
                                                       rhs=W.t[:, kc, hb * 512:(hb + 1) * 512],
                                                       start=(kc == 0), stop=(kc == 7)), reads=[dT, W], writes=[pacc])
            fw.op("act", lambda e: e.activation(out=L["ma"].t[:], in_=L["ma"].t[:], func=AF.Sigmoid),
                  reads=[L["ma"]], writes=[L["ma"]])
            fw.op("act", lambda e: e.activation(out=L["mb"].t[:], in_=L["mb"].t[:], func=AF.Sigmoid),
                  reads=[L["mb"]], writes=[L["mb"]])
            fw.op("dve", lambda e: e.tensor_tensor(out=L["ma"].t[:], in0=pa.t[:], in1=L["ma"].t[:], op=ALU.mult),
                  reads=[pa, L["ma"]], writes=[L["ma"]])
            fw.op("dve", lambda e: e.tensor_tensor(out=L["mb"].t[:], in0=pbb.t[:], in1=L["mb"].t[:], op=ALU.mult),
                  reads=[pbb, L["mb"]], writes=[L["mb"]])
            fw.op("pool", lambda e: e.tensor_tensor(out=mgbf.t[:], in0=L["ma"].t[:], in1=L["mb"].t[:], op=ALU.add),
                  reads=[L["ma"], L["mb"]], writes=[mgbf])
            for kc in range(8):
                fw.op("pe", lambda e: e.transpose(out=ptp.t[:, kc, :], in_=mgbf.t[:, kc * 128:(kc + 1) * 128],
                                                  identity=ident.t[:]), reads=[mgbf, ident], writes=[ptp])
            fw.op("act", lambda e: e.copy(out=mgT.t[:], in_=ptp.t[:]), reads=[ptp], writes=[mgT])
            for hb in range(2):
                for kc in range(8):
                    fw.op("pe", lambda e: e.matmul(po.t[:, hb * 512:(hb + 1) * 512], lhsT=mgT.t[:, kc, :],
                                                   rhs=Wb["o"].t[:, kc, hb * 512:(hb + 1) * 512],
                                                   start=(kc == 0), stop=(kc == 7)), reads=[mgT, Wb["o"]], writes=[po])
            fw.op("dve", lambda e: e.tensor_tensor(out=xo.t[:], in0=po.t[:], in1=L["xt"].t[:], op=ALU.add),
                  reads=[po, L["xt"]], writes=[xo])
            if not last:
                fw.dma(xnext.ap()[rows, :], xo.t[:], xo, False)
            else:
                fw.op("pool", lambda e: e.memset(fss.t[:], 0.0), writes=[fss])
                fw.op("act", lambda e: e.activation(out=sq.t[:], in_=xo.t[:], func=AF.Square, accum_out=fss.t[:]),
                      reads=[xo, fss], writes=[sq, fss])
                rstd_from(fss, D, NORM_EPS)
                fw.op("dve", lambda e: e.scalar_tensor_tensor(out=xo.t[:], in0=xo.t[:], scalar=fss.t[:, 0:1],
                                                              in1=fgb.t[:], op0=ALU.mult, op1=ALU.mult),
                      reads=[xo, fss, fgb], writes=[xo])
                fw.dma(out.ap()[rows, :], xo.t[:], xo, False)
            if dbg and n_layers < DEPTH and l == n_layers - 1:
                fw.dma(out.ap()[rows, :], xo.t[:], xo, False)
        fw.end_phase()

    print(f"[build] ins={fw.n_ins} waits={fw.n_wait} sems={fw.nsem}", flush=True)
    return nc


_INPUT_NAMES = ["norm_gain", "w_in", "w_vres_down", "shift_prev", "shift_next", "w_decay_up", "decay_bias",
                "w_iclr_up", "iclr_bias", "w_vres_up", "vres_bias", "k_k", "k_a", "r_k", "lnx_gain", "lnx_bias",
                "w_branch_a", "ret_norm_gain", "w_branch_b", "w_out", "final_gain"]


def make_in_maps(inputs):
    x = np.ascontiguousarray(np.asarray(inputs["x"], dtype=np.float32))
    shared = {}
    for nm in _INPUT_NAMES:
        a = np.ascontiguousarray(np.asarray(inputs[nm], dtype=np.float32))
        if nm == "r_k":
            a = a.reshape(DEPTH, D)
        if nm == "final_gain":
            a = a.reshape(1, D)
        shared[nm] = a
    in_maps = []
    for c in range(NCORE):
        m = dict(shared)
        m["x"] = x[c * NSEQ:(c + 1) * NSEQ].reshape(NTOK, D)
        in_maps.append(m)
    return in_maps


def kernel(**inputs):
    nc = build()
    in_maps = make_in_maps(inputs)
    res = run_bass_kernel_spmd(nc, in_maps, core_ids=list(range(NCORE)))
    outs = [np.asarray(r["out"], dtype=np.float32).reshape(NSEQ, SEQ, D) for r in res.results]
    return np.concatenate(outs, axis=0)
```
